# Optimizing a Trainium2 kernel written in Bass

```python
import math
import jax, jax.numpy as jnp
from jax import lax
import numpy as np

D_MODEL = 1024
BATCH = 2
SEQ = 8192
DEPTH = 1

HG_WIDTH = 512
HG_HEAD_DIM = 128
HG_HEADS = HG_WIDTH // HG_HEAD_DIM
HG_CHUNK = 64
S5_WIDTH = 512
S5_GROUP = 16
S5_GROUPS = S5_WIDTH // S5_GROUP
S5_STATE = 64
DT_MIN = 1e-3
DT_MAX = 1e-1
D_FF = 2816
CONV_WIDTH = 3
PLE_DIM = 256
N_BRANCH = 2
N_IN = 4 * HG_WIDTH + S5_WIDTH + N_BRANCH * D_MODEL
SPLITS = (HG_WIDTH, 2 * HG_WIDTH, 3 * HG_WIDTH, 4 * HG_WIDTH,
          4 * HG_WIDTH + S5_WIDTH, 4 * HG_WIDTH + S5_WIDTH + D_MODEL)
EPS = 1e-6

kernel_name = "hybrid_hgrn2_s5_gated_block"


def _rmsnorm(x, g):
    xf = x.astype(jnp.float32)
    y = xf * lax.rsqrt(jnp.mean(xf * xf, axis=-1, keepdims=True) + EPS) * g.astype(jnp.float32)
    return y.astype(x.dtype)


def _hgrn2(q_raw, f_raw, i_raw, og_raw, lb, norm_g):
    bsz, seqlen, _ = q_raw.shape
    nc = seqlen // HG_CHUNK

    def heads(t):
        t = t.astype(jnp.float32).reshape(bsz, nc, HG_CHUNK, HG_HEADS, HG_HEAD_DIM)
        return t.transpose(0, 3, 1, 2, 4)

    lb = lb.astype(jnp.float32).reshape(HG_HEADS, 1, 1, HG_HEAD_DIM)
    f = lb + (1.0 - lb) * jax.nn.sigmoid(heads(f_raw))
    k = 1.0 - f
    q = jax.nn.silu(heads(q_raw))
    v = heads(i_raw)
    b = jnp.cumsum(jnp.log(f), axis=-2)
    b_ref = b[..., HG_CHUNK // 2:HG_CHUNK // 2 + 1, :]
    b_last = b[..., -1:, :]
    scores = jnp.einsum('bhncd,bhnsd->bhncs', q * jnp.exp(b - b_ref), k * jnp.exp(b_ref - b))
    causal = jnp.tril(jnp.ones((HG_CHUNK, HG_CHUNK), dtype=bool))
    scores = jnp.where(causal, scores, 0.0)
    o_intra = jnp.einsum('bhncs,bhnse->bhnce', scores, v)
    u_chunk = jnp.einsum('bhncd,bhnce->bhnde', k * jnp.exp(b_last - b), v)
    decay = jnp.exp(b_last[..., 0, :])

    def step(S, inp):
        dec, u = inp
        return dec[..., None] * S + u, S

    S0 = jnp.zeros((bsz, HG_HEADS, HG_HEAD_DIM, HG_HEAD_DIM), jnp.float32)
    _, S_prev = lax.scan(step, S0, (jnp.moveaxis(decay, 2, 0), jnp.moveaxis(u_chunk, 2, 0)))
    S_prev = jnp.moveaxis(S_prev, 0, 2)
    o_inter = jnp.einsum('bhncd,bhnde->bhnce', q * jnp.exp(b), S_prev)
    o = o_intra + o_inter
    o = o * lax.rsqrt(jnp.mean(o * o, axis=-1, keepdims=True) + EPS) * norm_g.astype(jnp.float32)
    o = o.transpose(0, 2, 3, 1, 4).reshape(bsz, seqlen, HG_WIDTH)
    return (o * jax.nn.silu(og_raw.astype(jnp.float32))).astype(q_raw.dtype)


def _complex_affine_combine(e1, e2):
    a1r, a1i, b1r, b1i = e1
    a2r, a2i, b2r, b2i = e2
    return (a2r * a1r - a2i * a1i,
            a2r * a1i + a2i * a1r,
            a2r * b1r - a2i * b1i + b2r,
            a2r * b1i + a2i * b1r + b2i)


def _s5(u_raw, lam_re, lam_im, log_dt, b_re, b_im, c_re, c_im, d_skip, glu_w, glu_b):
    bsz, seqlen, _ = u_raw.shape
    u = u_raw.astype(jnp.float32).reshape(bsz, seqlen, S5_GROUPS, S5_GROUP)
    lr = lam_re.astype(jnp.float32)
    li = lam_im.astype(jnp.float32)
    dt = jnp.exp(log_dt.astype(jnp.float32))[:, None]
    mag = jnp.exp(lr * dt)
    a_re = mag * jnp.cos(li * dt)
    a_im = mag * jnp.sin(li * dt)
    den = lr * lr + li * li
    coef_re = ((a_re - 1.0) * lr + a_im * li) / den
    coef_im = (a_im * lr - (a_re - 1.0) * li) / den
    br = b_re.astype(jnp.float32)
    bi = b_im.astype(jnp.float32)
    bbar_re = coef_re[..., None] * br - coef_im[..., None] * bi
    bbar_im = coef_re[..., None] * bi + coef_im[..., None] * br
    bu_re = jnp.einsum('gnp,blgp->blgn', bbar_re, u)
    bu_im = jnp.einsum('gnp,blgp->blgn', bbar_im, u)
    shape = bu_re.shape
    elems = (jnp.broadcast_to(a_re, shape), jnp.broadcast_to(a_im, shape), bu_re, bu_im)
    _, _, x_re, x_im = lax.associative_scan(_complex_affine_combine, elems, axis=1)
    y = (jnp.einsum('gpn,blgn->blgp', c_re.astype(jnp.float32), x_re)
         - jnp.einsum('gpn,blgn->blgp', c_im.astype(jnp.float32), x_im))
    y = y + d_skip.astype(jnp.float32).reshape(S5_GROUPS, S5_GROUP) * u
    y = y.reshape(bsz, seqlen, S5_WIDTH).astype(u_raw.dtype)
    g = jax.nn.gelu(y)
    return g * jax.nn.sigmoid(g @ glu_w + glu_b)


def _conv_gated_ffn(h, w_up, conv_w, conv_b, w_down):
    a = h @ w_up
    seqlen = a.shape[1]
    ap = jnp.pad(a, ((0, 0), (CONV_WIDTH - 1, 0), (0, 0)))
    c = conv_b + conv_w[0] * ap[:, 0:seqlen]
    for k in range(1, CONV_WIDTH):
        c = c + conv_w[k] * ap[:, k:k + seqlen]
    gate, val = jnp.split(c, 2, axis=-1)
    return (jax.nn.gelu(gate) * val) @ w_down


def setup_inputs(seed: int = 0) -> dict:
    key = jax.random.key(seed)
    ks = jax.random.split(key, 32)
    f32 = jnp.float32

    def nrm(k, shape, scale):
        return jax.random.normal(k, shape, f32) * scale

    def gain(k, shape):
        return 1.0 + 0.01 * jax.random.normal(k, shape, f32)

    n_idx = jnp.arange(S5_STATE, dtype=f32)
    lam_re = -0.5 + 0.01 * jax.random.normal(ks[4], (DEPTH, S5_GROUPS, S5_STATE), f32)
    lam_im = math.pi * n_idx + 0.01 * jax.random.normal(ks[5], (DEPTH, S5_GROUPS, S5_STATE), f32)
    log_dt = jax.random.uniform(ks[6], (DEPTH, S5_GROUPS), f32,
                                minval=math.log(DT_MIN), maxval=math.log(DT_MAX))
    return {
        "x": nrm(ks[0], (BATCH, SEQ, D_MODEL), 1.0),
        "p": nrm(ks[1], (DEPTH, BATCH, SEQ, PLE_DIM), 1.0),
        "norm_mix_g": gain(ks[2], (DEPTH, D_MODEL)),
        "w_in": nrm(ks[3], (DEPTH, D_MODEL, N_IN), D_MODEL ** -0.5),
        "hg_lb_logits": nrm(ks[7], (DEPTH + 1, HG_WIDTH), 0.1),
        "hg_norm_g": gain(ks[8], (DEPTH, HG_HEAD_DIM)),
        "s5_lambda_re": lam_re,
        "s5_lambda_im": lam_im,
        "s5_log_dt": log_dt,
        "s5_b_re": nrm(ks[9], (DEPTH, S5_GROUPS, S5_STATE, S5_GROUP), (2 * S5_GROUP) ** -0.5),
        "s5_b_im": nrm(ks[10], (DEPTH, S5_GROUPS, S5_STATE, S5_GROUP), (2 * S5_GROUP) ** -0.5),
        "s5_c_re": nrm(ks[11], (DEPTH, S5_GROUPS, S5_GROUP, S5_STATE), S5_STATE ** -0.5),
        "s5_c_im": nrm(ks[12], (DEPTH, S5_GROUPS, S5_GROUP, S5_STATE), S5_STATE ** -0.5),
        "s5_d": nrm(ks[13], (DEPTH, S5_WIDTH), 1.0),
        "s5_glu_w": nrm(ks[14], (DEPTH, S5_WIDTH, S5_WIDTH), S5_WIDTH ** -0.5),
        "s5_glu_b": nrm(ks[15], (DEPTH, S5_WIDTH), 0.01),
        "w_branch_hg": nrm(ks[16], (DEPTH, HG_WIDTH, D_MODEL), HG_WIDTH ** -0.5),
        "w_branch_s5": nrm(ks[17], (DEPTH, S5_WIDTH, D_MODEL), S5_WIDTH ** -0.5),
        "w_out": nrm(ks[18], (DEPTH, D_MODEL, D_MODEL), D_MODEL ** -0.5),
        "norm_ffn_g": gain(ks[19], (DEPTH, D_MODEL)),
        "w_up": nrm(ks[20], (DEPTH, D_MODEL, 2 * D_FF), D_MODEL ** -0.5),
        "conv_w": nrm(ks[21], (DEPTH, CONV_WIDTH, 2 * D_FF), CONV_WIDTH ** -0.5),
        "conv_b": nrm(ks[22], (DEPTH, 2 * D_FF), 0.01),
        "w_down": nrm(ks[23], (DEPTH, D_FF, D_MODEL), D_FF ** -0.5),
        "norm_ple_g": gain(ks[24], (DEPTH, D_MODEL)),
        "w_ple_gate": nrm(ks[25], (DEPTH, D_MODEL, D_MODEL), D_MODEL ** -0.5),
        "w_ple_proj": nrm(ks[26], (DEPTH, PLE_DIM, D_MODEL), PLE_DIM ** -0.5),
        "norm_final_g": gain(ks[27], (D_MODEL,)),
    }


def reference(x, p, norm_mix_g, w_in, hg_lb_logits, hg_norm_g, s5_lambda_re, s5_lambda_im,
              s5_log_dt, s5_b_re, s5_b_im, s5_c_re, s5_c_im, s5_d, s5_glu_w, s5_glu_b,
              w_branch_hg, w_branch_s5, w_out, norm_ffn_g, w_up, conv_w, conv_b, w_down,
              norm_ple_g, w_ple_gate, w_ple_proj, norm_final_g):
    lbs = jnp.cumsum(jax.nn.softmax(hg_lb_logits.astype(jnp.float32), axis=0), axis=0)
    for i in range(DEPTH):
        h = _rmsnorm(x, norm_mix_g[i])
        proj = h @ w_in[i]
        q_raw, f_raw, i_raw, og_raw, u_raw, gate_hg, gate_s5 = jnp.split(proj, SPLITS, axis=-1)
        y_hg = _hgrn2(q_raw, f_raw, i_raw, og_raw, lbs[i], hg_norm_g[i]) @ w_branch_hg[i]
        y_s5 = _s5(u_raw, s5_lambda_re[i], s5_lambda_im[i], s5_log_dt[i], s5_b_re[i], s5_b_im[i],
                   s5_c_re[i], s5_c_im[i], s5_d[i], s5_glu_w[i], s5_glu_b[i]) @ w_branch_s5[i]
        merged = jax.nn.sigmoid(gate_hg) * y_hg + jax.nn.sigmoid(gate_s5) * y_s5
        x = x + merged @ w_out[i]
        x = x + _conv_gated_ffn(_rmsnorm(x, norm_ffn_g[i]), w_up[i], conv_w[i], conv_b[i], w_down[i])
        ple_gate = jax.nn.sigmoid(_rmsnorm(x, norm_ple_g[i]) @ w_ple_gate[i])
        x = x + ple_gate * (p[i] @ w_ple_proj[i])
    return _rmsnorm(x, norm_final_g)
```

```python
import contextlib
import numpy as np
import concourse.bass as bass
import concourse.mybir as mybir
from concourse.bass_utils import run_bass_kernel_spmd

F32 = mybir.dt.float32
BF16 = mybir.dt.bfloat16
AF = mybir.ActivationFunctionType
ALU = mybir.AluOpType

NCORES = 8
TOK = 2048
TT = 512
NT = TOK // TT
EPS = 1e-6
HALF_PI = 1.5707963267948966

C_ID, C_MQ, C_TRI, C_IND, C_NMQ, C_ML, C_RM, C_ONES, C_N = 0, 128, 256, 384, 392, 520, 648, 656, 784
P_GMIX, P_GFFN, P_GPLE, P_GFIN, P_HGG, P_GLUB, P_DSK, P_CW, P_CB, P_META = 0, 8, 16, 24, 32, 33, 37, 41, 173, 217
P_N = 228
Q_L0, Q_L1, Q_LRE, Q_LIM, Q_LDT, Q_BRE, Q_BIM, Q_CRE, Q_CIM, Q_N = 0, 512, 1024, 1040, 1056, 1072, 1328, 1584, 1840, 2096
XC = 548

GRAN = 256
WAITLOG = []


class Op:
    __slots__ = ("eng", "fn", "deps", "signal", "sig_no", "dma_slot", "dma_val", "is_dma", "inc", "rk", "wk", "seq")


class Prog:
    ENGS = ("pe", "act", "dve", "pool", "sp")

    def __init__(self, nc):
        self.nc = nc
        self.ops = {e: [] for e in self.ENGS}
        self.all = []
        self.state = {}

    def _keys(self, ap):
        t = ap.tensor
        name = t.name
        sp = str(ap.space).upper()
        if not ("SB" in sp or "PSUM" in sp):
            return [(name, 0, 0)]
        F = 1
        for s in t.shape[1:]:
            F *= s
        esz = 4 if ap.dtype == F32 else 2
        tsz = 4 if t.dtype == F32 else 2
        Fe = F * tsz // esz
        off = ap.offset
        p0 = off // Fe
        f0 = off % Fe
        apl = ap.ap
        pstep, pcnt = apl[0]
        if pstep == 0:
            pcnt = 1
        ext = 0
        for st, cn in apl[1:]:
            ext += abs(st) * (cn - 1)
        lo = f0 * esz
        hi = (f0 + ext + 1) * esz
        ks = []
        is_psum = "PSUM" in sp
        for q in range(p0 // 32, (p0 + pcnt - 1) // 32 + 1):
            if is_psum:
                ks.append((name, q, 0))
                continue
            for g in range(lo // GRAN, (hi - 1) // GRAN + 1):
                ks.append((name, q, g))
        return ks

    def add(self, eng, fn, reads=(), writes=(), is_dma=False, dma_slot=None, inc=16):
        op = Op()
        op.inc = inc
        op.eng = eng
        op.fn = fn
        op.is_dma = is_dma
        op.signal = False
        op.deps = set()
        op.sig_no = None
        op.dma_slot = dma_slot
        op.dma_val = None
        op.seq = len(self.all)
        op.rk = []
        op.wk = []
        st = self.state
        writes = list(writes) + [a for a in reads if "PSUM" in str(a.space).upper()]
        reads = [a for a in reads if "PSUM" not in str(a.space).upper()]
        for a in reads:
            for k in self._keys(a):
                op.rk.append(k)
                s = st.get(k)
                if s is None:
                    s = st[k] = [None, []]
                if s[0] is not None:
                    op.deps.add(s[0])
                s[1].append(op)
        for a in writes:
            for k in self._keys(a):
                op.wk.append(k)
                s = st.get(k)
                if s is None:
                    s = st[k] = [None, []]
                if s[0] is not None:
                    op.deps.add(s[0])
                for r in s[1]:
                    op.deps.add(r)
                s[0] = op
                s[1] = []
        op.deps.discard(op)
        self.ops[eng].append(op)
        self.all.append(op)
        return op

    def emit(self):
        nc = self.nc
        for op in self.all:
            nd = set()
            for d in op.deps:
                if (not d.is_dma) and (not op.is_dma) and d.eng == op.eng == "pe":
                    continue
                nd.add(d)
            last = {}
            keep = set()
            for d in nd:
                if d.is_dma:
                    keep.add(d)
                elif d.eng not in last or d.seq > last[d.eng].seq:
                    last[d.eng] = d
            keep.update(last.values())
            op.deps = keep
            for d in keep:
                d.signal = True
        for e in self.ENGS:
            n = 0
            for op in self.ops[e]:
                if (not op.is_dma) and op.signal:
                    n += 1
                    op.sig_no = n
        slot_cnt = {}
        for op in self.all:
            if op.is_dma:
                c = slot_cnt.get(op.dma_slot, 0) + op.inc
                slot_cnt[op.dma_slot] = c
                op.dma_val = c
        with contextlib.ExitStack() as es:
            esem = {e: es.enter_context(nc.semaphore("sem_" + e)) for e in self.ENGS}
            dsem = {k: es.enter_context(nc.semaphore("dsem_%d" % i)) for i, k in enumerate(slot_cnt)}
            block = es.enter_context(nc.Block())

            def run(ename, eng):
                waited = {}
                for op in self.ops[ename]:
                    need = {}
                    for d in op.deps:
                        if d.is_dma:
                            key = ("d", d.dma_slot)
                            val = d.dma_val
                            sem = dsem[d.dma_slot]
                        else:
                            key = ("e", d.eng)
                            val = d.sig_no
                            sem = esem[d.eng]
                        if val > need.get(key, (0, None))[0]:
                            need[key] = (val, sem)
                    for key, (val, sem) in need.items():
                        if waited.get(key, 0) >= val:
                            continue
                        eng.wait_ge(sem, val)
                        waited[key] = val
                        WAITLOG.append((ename, op.seq, key, val))
                    if op.is_dma and op.dma_val - op.inc > 0:
                        key = ("d", op.dma_slot)
                        if waited.get(key, 0) < op.dma_val - op.inc:
                            eng.wait_ge(dsem[op.dma_slot], op.dma_val - op.inc)
                            waited[key] = op.dma_val - op.inc
                    ins = op.fn(eng)
                    if op.is_dma:
                        ins.then_inc(dsem[op.dma_slot], op.inc)
                    elif op.signal:
                        ins.then_inc(esem[ename], 1)
                if ename == "sp":
                    for k, c in slot_cnt.items():
                        eng.wait_ge(dsem[k], c)

            @block.tensor
            def _(eng):
                run("pe", eng)

            @block.scalar
            def _(eng):
                run("act", eng)

            @block.vector
            def _(eng):
                run("dve", eng)

            @block.gpsimd
            def _(eng):
                run("pool", eng)

            @block.sync
            def _(eng):
                run("sp", eng)

    def dma(self, eng, out, in_, slot, **kw):
        return self.add(eng, lambda e: e.dma_start(out=out, in_=in_, **kw), reads=[in_], writes=[out],
                        is_dma=True, dma_slot=slot)

    def mm(self, out, lhsT, rhs, start=True, stop=True, **kw):
        return self.add("pe", lambda e: e.matmul(out, lhsT, rhs, start=start, stop=stop, **kw),
                        reads=[lhsT, rhs], writes=[out])

    def act(self, out, in_, func, **kw):
        rd = [in_] + [v for v in kw.values() if hasattr(v, "tensor")]
        return self.add("act", lambda e: e.activation(out=out, in_=in_, func=func, **kw), reads=rd, writes=[out])

    def tt(self, eng, out, in0, in1, op):
        return self.add(eng, lambda e: e.tensor_tensor(out=out, in0=in0, in1=in1, op=op), reads=[in0, in1], writes=[out])

    def ts(self, eng, out, in0, s1, s2, op0, op1=None):
        rd = [in0] + [v for v in (s1, s2) if hasattr(v, "tensor")]
        if op1 is None:
            return self.add(eng, lambda e: e.tensor_scalar(out=out, in0=in0, scalar1=s1, scalar2=None, op0=op0),
                            reads=rd, writes=[out])
        return self.add(eng, lambda e: e.tensor_scalar(out=out, in0=in0, scalar1=s1, scalar2=s2, op0=op0, op1=op1),
                        reads=rd, writes=[out])

    def stt(self, eng, out, in0, scalar, in1, op0, op1):
        rd = [in0, in1] + ([scalar] if hasattr(scalar, "tensor") else [])
        return self.add(eng, lambda e: e.scalar_tensor_tensor(out=out, in0=in0, scalar=scalar, in1=in1, op0=op0, op1=op1),
                        reads=rd, writes=[out])

    def copy(self, eng, out, in_):
        if eng == "act":
            return self.act(out, in_, AF.Copy)
        return self.add(eng, lambda e: e.tensor_copy(out=out, in_=in_), reads=[in_], writes=[out])

    def memset(self, eng, out, val):
        return self.add(eng, lambda e: e.memset(out, val), reads=[], writes=[out])

    def recip(self, out, in_):
        return self.add("dve", lambda e: e.reciprocal(out=out, in_=in_), reads=[in_], writes=[out])


class Arena:
    def __init__(self, nc, es, nbytes):
        self.t = es.enter_context(nc.sbuf_tensor("arena", [128, nbytes // 2], BF16))
        self.off = 0
        self.cap = nbytes

    def alloc(self, shape, dtype):
        esz = 4 if dtype == F32 else 2
        n = int(np.prod(shape))
        nb = n * esz
        al = GRAN if nb >= GRAN else 64
        self.off = (self.off + al - 1) // al * al
        o = self.off
        self.off += nb
        assert self.off <= self.cap, ("SBUF arena overflow", self.off, self.cap)
        v = self.t[:, o // 2:(o + nb) // 2]
        if dtype == F32:
            v = v.bitcast(F32)
        if len(shape) > 1:
            names = "abcde"[:len(shape)]
            pat = "p (" + " ".join(names) + ") -> p " + " ".join(names)
            v = v.rearrange(pat, **{names[i]: shape[i] for i in range(len(shape))})
        return v


def build(debug=()):
    nc = bass.Bass("TRN2", target_bir_lowering=False)
    dbg_out = {}

    def din(name, shape):
        return nc.dram_tensor(name, list(shape), F32, kind="ExternalInput").ap()

    xT = din("xT", [1024, TOK])
    pT = din("pT", [256, TOK])
    cst_d = din("cst", [128, C_N])
    prm_d = din("prm", [128, P_N])
    prq_d = din("prq", [128, Q_N])
    w_in = din("w_in", [1024, 4608])
    w_bh = din("w_bh", [512, 1024])
    w_bs = din("w_bs", [512, 1024])
    w_glu = din("w_glu", [512, 512])
    w_out = din("w_out", [1024, 1024])
    w_up = din("w_up", [1024, 5632])
    w_down = din("w_down", [2816, 1024])
    w_pg = din("w_pg", [1024, 1024])
    w_pp = din("w_pp", [256, 1024])
    outT = nc.dram_tensor("outT", [1024, TOK], F32, kind="ExternalOutput").ap()
    xmid_d = nc.dram_tensor("xmid", [1024, TOK], F32, kind="Internal").ap()
    xsrc_d = nc.dram_tensor("xsrc", [128, XC], F32, kind="Internal").ap()
    xdst_d = nc.dram_tensor("xdst", [4 * 128, XC], F32, kind="Internal").ap()
    hsrc_d = nc.dram_tensor("hsrc", [128, 16], F32, kind="Internal").ap()
    hdst_d = nc.dram_tensor("hdst", [4 * 128, 16], F32, kind="Internal").ap()

    w_in_v = w_in.rearrange("(k p) n -> p k n", p=128)
    w_out_v = w_out.rearrange("(k p) n -> p k n", p=128)
    w_up_v = w_up.rearrange("(k p) n -> p k n", p=128)
    w_down_v = w_down.rearrange("(k p) n -> p k n", p=128)
    w_pg_v = w_pg.rearrange("(k p) n -> p k n", p=128)
    w_pp_v = w_pp.rearrange("(k p) n -> p k n", p=128)
    w_bh_v = w_bh.rearrange("(k p) n -> p k n", p=128)
    w_bs_v = w_bs.rearrange("(k p) n -> p k n", p=128)
    w_glu_v = w_glu.rearrange("(k p) n -> p k n", p=128)
    xT_v = xT.rearrange("(k p) t -> p k t", p=128)
    pT_v = pT.rearrange("(k p) t -> p k t", p=128)
    xmid_v = xmid_d.rearrange("(k p) t -> p k t", p=128)
    outT_v = outT.rearrange("(k p) t -> p k t", p=128)

    with contextlib.ExitStack() as es:
        A = Arena(nc, es, 207 * 1024)
        psb = [es.enter_context(nc.psum_tensor("ps%d" % i, [128, 512], F32)) for i in range(8)]
        P = Prog(nc)
        bank_ctr = [0]

        def bank():
            b = psb[bank_ctr[0] % 6]
            bank_ctr[0] += 1
            return b[:, :]

        def dbg(name, ap):
            if name not in debug:
                return
            shp = [int(s) for s in ap.shape]
            d = nc.dram_tensor("dbg_" + name, shp, ap.dtype, kind="ExternalOutput").ap()
            P.dma("sp", d, ap, "dbg")
            dbg_out[name] = shp

        cst = A.alloc([C_N], F32)
        prm = A.alloc([P_N], F32)
        ident_f = cst[:, C_ID:C_ID + 128]
        CM_f = cst[:, C_MQ:C_MQ + 256]
        TRI_f = cst[:, C_TRI:C_TRI + 128]
        IND_f = cst[:, C_IND:C_IND + 2]
        NMQ_f = cst[:, C_NMQ:C_NMQ + 128]
        ML_f = cst[:, C_ML:C_ML + 128]
        RM_f = cst[:, C_RM:C_RM + 4]
        ident_b = A.alloc([128], BF16)
        ones_b = A.alloc([128], BF16)
        wbufs = [A.alloc([4096], BF16) for _ in range(3)]
        xt_off = (A.off + GRAN - 1) // GRAN * GRAN
        xt = A.alloc([8, TT], F32)
        xt_end = A.off
        A.off = xt_off
        gf = A.alloc([4, TT], F32)
        gb = A.alloc([4, TT], BF16)
        Xpb = A.alloc([16, 2, 64], BF16)
        assert A.off <= xt_end
        A.off = xt_end
        sq = A.alloc([8, TT], BF16)
        hT = A.alloc([8, TT], BF16)
        rs = A.alloc([TT], F32)
        b2_base = A.off
        LB = A.alloc([512], F32)
        OML = A.alloc([512], F32)
        S = A.alloc([4, 128], F32)
        SbA = A.alloc([4, 128], BF16)
        SbB = A.alloc([4, 128], BF16)
        logD = A.alloc([4], F32)
        dec = A.alloc([4, 2], F32)
        BD = A.alloc([4, 8, 128], BF16)
        WV = A.alloc([4, 8, 2, 128], BF16)
        WZ = A.alloc([16, 8, 2, 32], BF16)
        Ec = A.alloc([16, 64], F32)
        Es = A.alloc([16, 64], F32)
        Rtab = A.alloc([16, 64], F32)
        A8 = A.alloc([2, 16], F32)
        A2048 = A.alloc([2, 16], F32)
        Xcar = A.alloc([2, 16], F32)
        wbh = A.alloc([4, 1024], BF16)
        wbs = A.alloc([4, 1024], BF16)
        gluw = A.alloc([4, 512], BF16)
        mark = A.off

        gcol = lambda o: prm[:, o:o + 8]
        hgg = prm[:, P_HGG:P_HGG + 1]
        glub = prm[:, P_GLUB:P_GLUB + 4]
        dsk = prm[:, P_DSK:P_DSK + 4]
        convw = prm[:, P_CW:P_CW + 132].rearrange("p (k j) -> p k j", k=3)
        convb = prm[:, P_CB:P_CB + 44]
        meta = prm[:, P_META:P_META + 8]

        P.dma("sp", cst, cst_d, "c0")
        P.dma("sp", prm, prm_d, "c0")
        P.copy("dve", ident_b, ident_f)
        P.copy("dve", ones_b, cst[:, C_ONES:C_ONES + 128])
        P.dma("pool", wbh, w_bh_v, "wres")
        P.dma("pool", wbs, w_bs_v, "wres")
        P.dma("pool", gluw, w_glu_v, "wres")

        class WQ:
            def __init__(self):
                self.specs = []
                self.views = {}
                self.i = 0
                self.issued = 0
                self.bufs = wbufs
                self.gen = 0

            def set_bufs(self, bufs):
                self.bufs = bufs
                self.gen += 1

            def plan(self, parts, shape):
                self.specs.append((parts, shape))

            def _issue(self, k):
                parts, shape = self.specs[k]
                nb = len(self.bufs)
                buf = self.bufs[k % nb]
                n = int(np.prod(shape))
                v = buf[:, :n]
                names = "abcd"[:len(shape)]
                v = v.rearrange("p (" + " ".join(names) + ") -> p " + " ".join(names),
                                **{names[i]: shape[i] for i in range(len(shape))})
                for pi_, (sel, src) in enumerate(parts):
                    P.dma("pool", sel(v), src, "w%d_%d_%d" % (self.gen, k % nb, pi_))
                self.views[k] = v

            def next(self):
                while self.issued < min(len(self.specs), self.i + len(self.bufs) - 1):
                    self._issue(self.issued)
                    self.issued += 1
                v = self.views.pop(self.i)
                self.i += 1
                return v

        wq = WQ()
        FGRP = [(0, 4), (4, 4), (8, 4), (12, 4), (16, 4), (20, 2)]
        whole = lambda v: v
        for t in range(NT):
            for c0 in (512, 1024, 2048):
                wq.plan([(whole, w_in_v[:, :, c0:c0 + 512])], [8, 512])
        for t in range(NT):
            for c0 in (2048, 0, 1536, 512, 1024, 2560, 3584, 3072, 4096):
                wq.plan([(whole, w_in_v[:, :, c0:c0 + 512])], [8, 512])
            for c0 in (0, 512):
                wq.plan([(whole, w_out_v[:, :, c0:c0 + 512])], [8, 512])
        for t in range(NT):
            for (j0, nj) in FGRP:
                wq.plan([(whole, w_up_v[:, :, j0 * 128:(j0 + nj) * 128])], [8, nj * 128])
                wq.plan([(whole, w_up_v[:, :, 2816 + j0 * 128:2816 + (j0 + nj) * 128])], [8, nj * 128])
            for (j0, nj) in FGRP:
                wq.plan([(whole, w_down_v[:, j0:j0 + nj, :])], [nj, 1024])
            for c0 in (0, 512):
                wq.plan([(whole, w_pg_v[:, :, c0:c0 + 512])], [8, 512])

        def rmsnorm(x, g8, out, n, sq, rs):
            ss = bank()[:, :n]
            for k in range(8):
                if n >= 64:
                    P.act(sq[:, k, :], x[:, k, :], AF.Square)
                elif k == 0:
                    P.act(sq, x, AF.Square)
                P.mm(ss, ones_b, sq[:, k, :], start=(k == 0), stop=(k == 7))
            P.act(rs, ss, AF.Ln, scale=1.0 / 1024, bias=EPS)
            P.act(rs, rs, AF.Exp, scale=-0.5)
            for k in range(8):
                P.stt("dve", out[:, k, :], x[:, k, :], g8[:, k:k + 1], rs, ALU.mult, ALU.mult)

        def cmul(eng, outr, outi, ar, ai, br, bi, t1, t2):
            P.tt(eng, t1, ar, br, ALU.mult)
            P.tt(eng, t2, ai, bi, ALU.mult)
            P.tt(eng, outr, t1, t2, ALU.subtract)
            P.tt(eng, t1, ar, bi, ALU.mult)
            P.tt(eng, t2, ai, br, ALU.mult)
            P.tt(eng, outi, t1, t2, ALU.add)

        def csq(eng, r, i, t1, t2):
            P.tt(eng, t1, r, r, ALU.mult)
            P.tt(eng, t2, i, i, ALU.mult)
            P.stt(eng, i, r, 2.0, i, ALU.mult, ALU.mult)
            P.tt(eng, r, t1, t2, ALU.subtract)

        A.off = mark
        prq = A.alloc([Q_N], F32)
        P.dma("sp", prq, prq_d, "c1")
        P.tt("dve", LB, prq[:, Q_L0:Q_L0 + 512], prq[:, Q_L1:Q_L1 + 512], ALU.subtract)
        P.act(LB, LB, AF.Sigmoid)
        P.ts("dve", OML, LB, -1.0, 1.0, ALU.mult, ALU.add)
        lre = prq[:, Q_LRE:Q_LRE + 16]
        lim = prq[:, Q_LIM:Q_LIM + 16]
        ldt = prq[:, Q_LDT:Q_LDT + 16]
        Bre = prq[:, Q_BRE:Q_BRE + 256].rearrange("p (q c) -> p q c", q=16)
        Bim = prq[:, Q_BIM:Q_BIM + 256].rearrange("p (q c) -> p q c", q=16)
        Cre = prq[:, Q_CRE:Q_CRE + 256].rearrange("p (q c) -> p q c", q=16)
        Cim = prq[:, Q_CIM:Q_CIM + 256].rearrange("p (q c) -> p q c", q=16)
        sm = A.alloc([24, 16], F32)
        dt_, lrdt, lidt, mag, ur, ui, t1, t2, are, aim, mag8, u8r, u8i, cr, ci, den, xx, pkr, pki = [sm[:, i, :] for i in range(19)]
        PW = A.alloc([9, 2, 16], F32)
        P.act(dt_, ldt, AF.Exp)
        P.tt("dve", lrdt, lre, dt_, ALU.mult)
        P.tt("dve", lidt, lim, dt_, ALU.mult)
        P.act(mag, lrdt, AF.Exp, scale=1.0 / 256)
        P.act(ui, lidt, AF.Sin, scale=1.0 / 256)
        P.act(ur, lidt, AF.Sin, scale=1.0 / 256, bias=HALF_PI)
        for _ in range(8):
            P.tt("dve", mag, mag, mag, ALU.mult)
            csq("dve", ur, ui, t1, t2)
        P.tt("dve", are, mag, ur, ALU.mult)
        P.tt("dve", aim, mag, ui, ALU.mult)
        P.copy("dve", mag8, mag)
        P.copy("dve", u8r, ur)
        P.copy("dve", u8i, ui)
        for _ in range(3):
            P.tt("dve", mag8, mag8, mag8, ALU.mult)
            csq("dve", u8r, u8i, t1, t2)
        P.memset("dve", PW[:, 0, 0, :], 1.0)
        P.memset("dve", PW[:, 0, 1, :], 0.0)
        P.copy("dve", PW[:, 1, 0, :], are)
        P.copy("dve", PW[:, 1, 1, :], aim)
        for m in range(2, 9):
            cmul("dve", PW[:, m, 0, :], PW[:, m, 1, :], PW[:, m - 1, 0, :], PW[:, m - 1, 1, :], are, aim, t1, t2)
        P.copy("dve", A8[:, 0, :], PW[:, 8, 0, :])
        P.copy("dve", A8[:, 1, :], PW[:, 8, 1, :])
        P.copy("dve", A2048[:, 0, :], PW[:, 8, 0, :])
        P.copy("dve", A2048[:, 1, :], PW[:, 8, 1, :])
        for _ in range(8):
            csq("dve", A2048[:, 0, :], A2048[:, 1, :], t1, t2)
        P.tt("dve", den, lre, lre, ALU.mult)
        P.tt("dve", t1, lim, lim, ALU.mult)
        P.tt("dve", den, den, t1, ALU.add)
        P.recip(den, den)
        P.ts("dve", xx, are, -1.0, None, ALU.add)
        P.tt("dve", t1, xx, lre, ALU.mult)
        P.tt("dve", t2, aim, lim, ALU.mult)
        P.tt("dve", cr, t1, t2, ALU.add)
        P.tt("dve", cr, cr, den, ALU.mult)
        P.tt("dve", t1, aim, lre, ALU.mult)
        P.tt("dve", t2, xx, lim, ALU.mult)
        P.tt("dve", ci, t1, t2, ALU.subtract)
        P.tt("dve", ci, ci, den, ALU.mult)
        big = A.alloc([8, 16, 16], F32)
        bbr, bbi, T1, T2, fr, fi = [big[:, i] for i in range(6)]
        bc = lambda v: v.rearrange("p (q o) -> p q o", o=1).broadcast_to([128, 16, 16])
        cmul("dve", bbr, bbi, bc(cr), bc(ci), Bre, Bim, T1, T2)
        P.memset("dve", Ec[:, :, 0:1], 1.0)
        P.memset("dve", Es[:, :, 0:1], 0.0)
        P.copy("dve", pkr, u8r)
        P.copy("dve", pki, u8i)
        Tt = A.alloc([2, 16, 32], F32)
        for k in range(6):
            n = 1 << k
            bq = lambda v: v.rearrange("p (q o) -> p q o", o=1).broadcast_to([128, 16, n])
            cmul("dve", Ec[:, :, n:2 * n], Es[:, :, n:2 * n], Ec[:, :, 0:n], Es[:, :, 0:n], bq(pkr), bq(pki),
                 Tt[:, 0, :, 0:n], Tt[:, 1, :, 0:n])
            if k < 5:
                csq("dve", pkr, pki, t1, t2)
        P.copy("dve", Rtab, mag8.rearrange("p (q o) -> p q o", o=1).broadcast_to([128, 16, 64]))
        P.memset("dve", Rtab[:, :, 0:1], 0.0)
        WVs = A.alloc([2, 16, 32], F32)
        WVs7 = A.alloc([2, 16, 32], F32)
        Zbd = A.alloc([2, 16, 32], F32)
        BDf = A.alloc([8, 128], F32)
        P.memset("pool", WVs, 0.0)
        P.memset("pool", WVs7, 0.0)
        P.memset("pool", Zbd, 0.0)
        pwb = lambda m, ri: PW[:, m, ri, :].rearrange("p (q o) -> p q o", o=1).broadcast_to([128, 16, 16])

        def fill_bd(dst, vr, vi, neg_im, ctmajor=False):
            for h in range(2):
                ps_ = slice(64 * h, 64 * h + 64)
                cs_ = slice(16 * h, 16 * h + 16)
                for ri, v in ((0, vr), (1, vi)):
                    d_ = dst[ps_, ri, :, cs_]
                    s_ = v[ps_]
                    if ctmajor:
                        d_ = d_.rearrange("p (ct qp) c -> p qp ct c", ct=4)
                        s_ = s_.rearrange("p (qp ct) c -> p qp ct c", ct=4)
                    if ri == 1 and neg_im:
                        P.ts("dve", d_, s_, -1.0, None, ALU.mult)
                    else:
                        P.copy("dve", d_, s_)

        for i in (7, 6, 5, 4, 3, 2, 1, 0):
            dst = WVs7 if i == 7 else WVs
            cmul("dve", fr, fi, pwb(7 - i, 0), pwb(7 - i, 1), bbr, bbi, T1, T2)
            fill_bd(dst, fr, fi, False, ctmajor=True)
            pb0 = bank()
            pb1 = bank()
            for ct in range(4):
                for ri in range(2):
                    pb = pb0 if ct < 2 else pb1
                    o = ((ct % 2) * 2 + ri) * 128
                    src = dst[:, ri, 4 * ct:4 * ct + 4, :].rearrange("p q c -> p (q c)")
                    P.mm(pb[:, o:o + 128], src, ident_f)
            for hh, pb in enumerate((pb0, pb1)):
                P.copy("act", WV[:, 2 * hh:2 * hh + 2, i, :, :], pb.rearrange("p (c r n) -> p c r n", c=2, r=2))
        pBD = [bank() for _ in range(4)]
        for m in range(9):
            cmul("dve", fr, fi, pwb(m, 0), pwb(m, 1), Cre, Cim, T1, T2)
            fill_bd(Zbd, fr, fi, True)
            if m >= 1:
                P.copy("act", WZ[:, :, m - 1, :, :], Zbd.rearrange("p r q c -> p q r c"))
            if m <= 7:
                for ct in range(4):
                    for qp in range(4):
                        q = qp * 4 + ct
                        for ri in range(2):
                            P.mm(pBD[ct][32 * qp:32 * qp + 32, m * 32:(m + 1) * 32], WVs7[:, ri, ct * 4 + qp, :], Zbd[:, ri, q, :],
                                 start=(ri == 0), stop=(ri == 1), tile_position=(0, 32 * qp))
        for ct in range(4):
            pv = pBD[ct][:, 0:256].rearrange("p (m c) -> p m c", m=8)
            for qp in range(4):
                P.ts("dve", BDf[:, :, 32 * qp:32 * qp + 32], pv, RM_f[:, qp:qp + 1], None, ALU.mult)
            P.stt("dve", BDf[:, 0, :], ident_f, dsk[:, ct:ct + 1], BDf[:, 0, :], ALU.mult, ALU.add)
            P.copy("act", BD[:, ct], BDf)
        dbg("BD", BD)
        dbg("WV", WV)
        dbg("WZ", WZ)
        dbg("Ec", Ec)
        dbg("PW", PW)

        A.off = mark
        qT = A.alloc([4, TT], BF16)
        ogT = A.alloc([4, TT], BF16)
        uT = A.alloc([4, TT], BF16)
        yhg = A.alloc([4, TT], BF16)
        ys5 = A.alloc([4, TT], BF16)
        ov0 = A.off
        sig = A.alloc([512], F32)
        lf = A.alloc([512], F32)
        kk = A.alloc([512], F32)
        eT = A.alloc([512], F32)
        khat = A.alloc([512], BF16)
        ktil = A.alloc([512], BF16)
        vb = A.alloc([512], BF16)
        alt0 = A.off
        ebq = A.alloc([4, 256], F32)
        qtil = A.alloc([4, 128], BF16)
        qhat = A.alloc([4, 128], BF16)
        ktT = A.alloc([4, 128], BF16)
        scT = A.alloc([4, 128], BF16)
        osq = A.alloc([512], BF16)
        orstd = A.alloc([512], F32)
        oy = A.alloc([512], F32)
        ov1 = A.off
        A.off = alt0
        SET0 = (sig, lf, kk, eT, khat, vb)
        SET1 = (A.alloc([512], F32), A.alloc([512], F32), A.alloc([512], F32), A.alloc([512], F32),
                A.alloc([512], BF16), A.alloc([512], BF16))
        assert A.off <= ov1
        A.off = ov0
        wre = A.alloc([16, 64], F32)
        wim = A.alloc([16, 64], F32)
        tA = A.alloc([16, 64], F32)
        tB = A.alloc([16, 64], F32)
        zre = A.alloc([16, 64], F32)
        zim = A.alloc([16, 64], F32)
        cc = A.alloc([4, 16], F32)
        Xold = A.alloc([2, 16], F32)
        A.off = max(A.off, ov1)
        A_sgz_raw = A.alloc([2, TT], F32)
        sg = A_sgz_raw[:, 0, :]
        sg2 = A_sgz_raw[:, 1, :]
        A_sgz = A_sgz_raw.rearrange("p a b -> p (a b)").rearrange("p (q c) -> p q c", q=16)
        mt1 = A.alloc([TT], F32)
        merged = sq
        xg = A.alloc([4, XC], F32)
        hx = A.alloc([4, 16], F32)
        vb_alt = A.alloc([512], BF16)
        b1_end = A.off

        P.memset("dve", S, 0.0)
        P.memset("dve", logD, 0.0)
        P.memset("dve", Xcar, 0.0)

        def hgrn_front(wf, wi, bi, full):
            tok = slice(bi * 128, (bi + 1) * 128)
            sig, lf, kk, eT, khat, vb = SET0 if (full or bi % 2 == 0) else SET1
            if full and bi % 2 == 1:
                vb = vb_alt
            pf = bank()
            for k in range(8):
                P.mm(pf, hT[:, k, tok], wf[:, k, :], start=(k == 0), stop=(k == 7))
            pi_ = bank()
            for k in range(8):
                P.mm(pi_, hT[:, k, tok], wi[:, k, :], start=(k == 0), stop=(k == 7))
            P.act(sig, pf, AF.Sigmoid)
            P.tt("dve", sig, sig, OML, ALU.mult)
            P.tt("dve", sig, sig, LB, ALU.add)
            P.act(lf, sig, AF.Ln)
            P.act(kk, sig, AF.Identity, scale=-1.0, bias=1.0)
            P.act(vb, pi_, AF.Copy)
            return dict(tok=tok, bufs=(sig, lf, kk, eT, khat, vb), full=full)

        def hgrn_mid(c):
            tok, full = c["tok"], c["full"]
            sig, lf, kk, eT, khat, vb = c["bufs"]
            pL = bank()
            P.mm(pL, ML_f, lf)
            if full:
                pC = [bank(), bank()]
                for h in range(4):
                    P.mm(pC[h // 2][:, (h % 2) * 256:(h % 2 + 1) * 256], lf[:, h * 128:(h + 1) * 128], CM_f)
                pK = bank()
                P.mm(pK, NMQ_f, lf)
            pD = bank()
            pDv = pD[:, 0:8].rearrange("p (h j) -> p h j", h=4)
            for h in range(4):
                P.mm(pDv[:, h, :], lf[:, h * 128:(h + 1) * 128], IND_f)
            P.act(eT, pL, AF.Exp)
            P.tt("dve", khat, kk, eT, ALU.mult)
            pU = [psb[6][:, :], psb[7][:, :]]
            for j in range(2):
                for h in range(4):
                    hs = slice(h * 128, (h + 1) * 128)
                    P.mm(pU[j][:, hs], khat[64 * j:64 * j + 64, hs], vb[64 * j:64 * j + 64, hs])
            c["pUv"] = [p_.rearrange("p (h e) -> p h e", h=4) for p_ in pU]
            if full:
                P.act(eT, pK, AF.Exp)
                P.tt("dve", ktil, kk, eT, ALU.mult)
                pTr = bank()
                for h in range(4):
                    hs = slice(h * 128, (h + 1) * 128)
                    P.mm(pTr[:, hs], ktil[:, hs], ident_b)
                for hh in range(2):
                    P.act(ebq[:, 2 * hh:2 * hh + 2, :], pC[hh].rearrange("p (h c) -> p h c", h=2), AF.Exp)
                P.tt("dve", qtil, qT[:, :, tok], ebq[:, :, 0:128], ALU.mult)
                P.copy("act", ktT, pTr.rearrange("p (h c) -> p h c", h=4))
                pS = bank()
                for h in range(4):
                    P.mm(pS[:, h * 128:(h + 1) * 128], ktT[:, h, :], qtil[:, h, :])
                P.tt("dve", qhat, qT[:, :, tok], ebq[:, :, 128:256], ALU.mult)
                P.tt("dve", scT, pS.rearrange("p (h c) -> p h c", h=4),
                     TRI_f.rearrange("p (o c) -> p o c", o=1).broadcast_to([128, 4, 128]), ALU.mult)
            P.act(dec, pDv, AF.Exp)
            if not full:
                P.tt("dve", logD, logD, pDv[:, :, 0], ALU.add)
                P.tt("dve", logD, logD, pDv[:, :, 1], ALU.add)

        def hgrn_back(c):
            tok, full = c["tok"], c["full"]
            vb = c["bufs"][5]
            pUv = c["pUv"]
            decb = lambda j: dec[:, :, j:j + 1].broadcast_to([128, 4, 128])
            P.tt("dve", S, S, decb(0), ALU.mult)
            if full:
                P.tt("dve", SbB, S, pUv[0], ALU.add)
            P.tt("dve", S, S, pUv[0], ALU.add)
            if full:
                pO = bank()
                for h in range(4):
                    hs = slice(h * 128, (h + 1) * 128)
                    P.mm(pO[:, hs], vb[:, hs], scT[:, h, :], start=True, stop=False)
                    P.mm(pO[:, h * 128:h * 128 + 64], SbA[:, h, :], qhat[:, h, 0:64], start=False, stop=False)
                    P.mm(pO[:, h * 128 + 64:h * 128 + 128], SbB[:, h, :], qhat[:, h, 64:128], start=False, stop=True)
            P.tt("dve", S, S, decb(1), ALU.mult)
            if full:
                P.tt("dve", SbA, S, pUv[1], ALU.add)
            P.tt("dve", S, S, pUv[1], ALU.add)
            if full:
                P.act(osq, pO, AF.Square)
                P.stt("dve", oy.rearrange("p (h c) -> p h c", h=4), pO.rearrange("p (h c) -> p h c", h=4), hgg, ogT[:, :, tok],
                      ALU.mult, ALU.mult)
                pN = bank()
                P.mm(pN, ones_b, osq)
                P.act(orstd, pN, AF.Ln, scale=1.0 / 128, bias=EPS)
                P.act(orstd, orstd, AF.Exp, scale=-0.5)
                P.tt("dve", yhg[:, :, tok], oy.rearrange("p (h c) -> p h c", h=4),
                     orstd.rearrange("p (h c) -> p h c", h=4), ALU.mult)

        def hgrn_tile(wf, wi, full, hooks=None):
            c = hgrn_front(wf, wi, 0, full)
            for bi in range(4):
                hgrn_mid(c)
                nxt = hgrn_front(wf, wi, bi + 1, full) if bi < 3 else None
                hgrn_back(c)
                if hooks and bi in hooks:
                    hooks[bi]()
                c = nxt

        def s5_stages(U, T):
            wre, wim, tA, tB, zre, zim, cc, Xold = T
            uv = U.rearrange("p t (c i) -> p t i c", i=8)
            st = {}

            def stage_v():
                pV = [bank() for _ in range(4)]
                st["pVv"] = [p_.rearrange("p (t r c) -> p t r c", t=4, r=2) for p_ in pV]
                for ct in range(4):
                    for ri in range(2):
                        for i in range(8):
                            for qp in range(4):
                                rsl = slice(32 * qp, 32 * qp + 32)
                                P.mm(st["pVv"][qp][:, ct, ri, :], WV[rsl, ct, i, ri, :], uv[rsl, ct, i, :],
                                     start=(i == 0), stop=(i == 7), tile_position=(32 * qp, 0))

            def stage_rotin():
                pVv = st["pVv"]
                for qp in range(4):
                    sl = slice(4 * qp, 4 * qp + 4)
                    vr = pVv[qp][:, :, 0, :]
                    vi = pVv[qp][:, :, 1, :]
                    P.tt("dve", wre[:, sl, :], vr, Ec[:, sl, :], ALU.mult)
                    P.tt("dve", tA[:, sl, :], vi, Es[:, sl, :], ALU.mult)
                    P.tt("dve", wim[:, sl, :], vi, Ec[:, sl, :], ALU.mult)
                    P.tt("dve", tB[:, sl, :], vr, Es[:, sl, :], ALU.mult)
                P.tt("dve", wre, wre, tA, ALU.add)
                P.tt("dve", wim, wim, tB, ALU.subtract)

            def stage_scan():
                P.copy("dve", Xold, Xcar)
                cmul("dve", cc[:, 0, :], cc[:, 1, :], A8[:, 0, :], A8[:, 1, :], Xold[:, 0, :], Xold[:, 1, :], cc[:, 2, :], cc[:, 3, :])
                P.tt("dve", wre[:, :, 0], wre[:, :, 0], cc[:, 0, :], ALU.add)
                P.tt("dve", wim[:, :, 0], wim[:, :, 0], cc[:, 1, :], ALU.add)
                fl = lambda v: v.rearrange("p q c -> p (q c)")

                def scan(z_, w_):
                    P.add("dve", lambda e: e.tensor_tensor_scan(out=fl(z_), data0=fl(Rtab), data1=fl(w_), initial=0.0,
                                                                 op0=ALU.mult, op1=ALU.add), reads=[Rtab, w_], writes=[z_])
                scan(zre, wre)
                scan(zim, wim)

            def stage_rotout():
                P.tt("dve", wre, zre, Ec, ALU.mult)
                P.tt("dve", tA, zim, Es, ALU.mult)
                P.tt("dve", wim, zim, Ec, ALU.mult)
                P.tt("dve", tB, zre, Es, ALU.mult)
                P.tt("dve", wre, wre, tA, ALU.subtract)
                P.tt("dve", wim, wim, tB, ALU.add)
                P.copy("dve", Xcar[:, 0, :], wre[:, :, 63])
                P.copy("dve", Xcar[:, 1, :], wim[:, :, 63])

            return stage_v, stage_rotin, stage_scan, stage_rotout

        TB = (wre, wim, tA, tB, zre, zim, cc, Xold)

        def s5_chain():
            for f_ in s5_stages(uT, TB):
                f_()
            P.copy("act", Xpb[:, :, 0, 1:64], wre[:, :, 0:63])
            P.copy("act", Xpb[:, :, 1, 1:64], wim[:, :, 0:63])
            P.copy("dve", Xpb[:, :, 0, 0], Xold[:, 0, :])
            P.copy("dve", Xpb[:, :, 1, 0], Xold[:, 1, :])

        def s5_out():
            uv = uT.rearrange("p t (c i) -> p t i c", i=8)
            for ct in range(4):
                pY = bank()
                pYv = pY.rearrange("p (i c) -> p i c", i=8)
                for tau in range(8):
                    P.mm(pYv[:, tau:8, :], BD[:, ct, tau, :], uv[:, ct, 0:8 - tau, :], start=(tau == 0), stop=False)
                for i in range(8):
                    for ri in range(2):
                        for qp in range(4):
                            q = qp * 4 + ct
                            P.mm(pYv[32 * qp:32 * qp + 32, i, :], WZ[:, q, i, ri, :], Xpb[:, q, ri, :],
                                 start=False, stop=(i == 7 and ri == 1), tile_position=(0, 32 * qp))
                P.act(gf[:, ct, :].rearrange("p (c i) -> p i c", i=8), pYv, AF.Gelu_apprx_tanh)
                P.copy("dve", gb[:, ct, :], gf[:, ct, :])
            for m in range(4):
                pG = bank()
                for ct in range(4):
                    P.mm(pG, gluw[:, ct, m * 128:(m + 1) * 128], gb[:, ct, :], start=(ct == 0), stop=(ct == 3))
                P.act(sg, pG, AF.Sigmoid, bias=glub[:, m:m + 1])
                P.tt("dve", ys5[:, m, :], gf[:, m, :], sg, ALU.mult)

        def load_x(t):
            for hh in range(2):
                P.dma("sp", xt[:, 4 * hh:4 * hh + 4, :], xT_v[:, 4 * hh:4 * hh + 4, t * TT:(t + 1) * TT], "x%d" % hh)

        f32v = lambda v, shp: v.rearrange("p a b -> p (a b)").bitcast(F32).rearrange("p (q c) -> p q c", q=shp[0])
        xgf = xg.rearrange("p a b -> p (a b)")
        TA = (f32v(qT, [16, 64]), f32v(ogT, [16, 64]), f32v(yhg, [16, 64]),
              xgf[:, 0:1024].rearrange("p (q c) -> p q c", q=16), xgf[:, 1024:2048].rearrange("p (q c) -> p q c", q=16),
              A_sgz, mt1[:, 0:64].rearrange("p (a b) -> p a b", a=4), mt1[:, 64:96].rearrange("p (a b) -> p a b", a=2))
        uTA = [uT, ys5]
        pend = None
        for t in range(NT):
            load_x(t)
            rmsnorm(xt, gcol(P_GMIX), hT, TT, sq, rs)
            wf = wq.next()
            wi = wq.next()
            hooks = None
            if pend:
                pend[0]()
                pend[1]()
                hooks = {0: pend[2], 1: pend[3]}
            hgrn_tile(wf, wi, False, hooks)
            wu = wq.next()
            U = uTA[t % 2]
            for m in range(4):
                pu = bank()
                for k in range(8):
                    P.mm(pu, wu[:, k, m * 128:(m + 1) * 128], hT[:, k, :], start=(k == 0), stop=(k == 7))
                P.copy("act", U[:, m, :], pu)
            pend = s5_stages(U, TA)
        def b1_prefix(t):
            load_x(t)
            rmsnorm(xt, gcol(P_GMIX), hT, TT, sq, rs)
            wu = wq.next()
            for m in range(4):
                pu = bank()
                for k in range(8):
                    P.mm(pu, wu[:, k, m * 128:(m + 1) * 128], hT[:, k, :], start=(k == 0), stop=(k == 7))
                P.copy("act", uT[:, m, :], pu)

        b1_prefix(0)
        for f_ in pend:
            f_()

        xs = gf[:, 0:2, :].rearrange("p a b -> p (a b)")[:, 0:XC]
        P.copy("dve", xs[:, 0:512], S.rearrange("p h e -> p (h e)"))
        P.copy("dve", xs[:, 512:516], logD)
        P.copy("dve", xs[:, 516:548], Xcar.rearrange("p r q -> p (r q)"))
        dbg("xs", xs)
        P.dma("sp", xsrc_d, xs, "xsrc")
        P.add("pool", lambda e: e.collective_compute("AllGather", ALU.bypass, replica_groups=[[0, 1, 2, 3], [4, 5, 6, 7]],
                                                      ins=[xsrc_d], outs=[xdst_d]), reads=[xsrc_d], writes=[xdst_d],
              is_dma=True, dma_slot="cc1", inc=1)
        P.dma("sp", xg, xdst_d.rearrange("(r p) f -> p r f", p=128), "xg")
        def combine_states():
            P.memset("dve", S, 0.0)
            P.memset("dve", Xcar, 0.0)
            om = cc[:, 0, 0:4]
            Dr = cc[:, 1, 0:4]
            Ar = cc[:, 2, 0:4]
            P.ts("dve", om, meta[:, 0:4], -1.0, 1.0, ALU.mult, ALU.add)
            for r in range(4):
                mr = meta[:, r:r + 1]
                P.act(Dr, xg[:, r, 512:516], AF.Exp)
                P.ts("dve", Ar, Dr, mr, om[:, r:r + 1], ALU.mult, ALU.add)
                P.ts("dve", sig, xg[:, r, 0:512], mr, None, ALU.mult)
                for h in range(4):
                    P.stt("dve", S[:, h, :], S[:, h, :], Ar[:, h:h + 1], sig[:, h * 128:(h + 1) * 128], ALU.mult, ALU.add)
                cmul("dve", Xold[:, 0, :], Xold[:, 1, :], A2048[:, 0, :], A2048[:, 1, :], Xcar[:, 0, :], Xcar[:, 1, :],
                     cc[:, 3, :], lf[:, 0:16])
                for ri in range(2):
                    P.tt("dve", Xold[:, ri, :], Xold[:, ri, :], xg[:, r, 516 + 16 * ri:532 + 16 * ri], ALU.add)
                    P.tt("dve", Xold[:, ri, :], Xold[:, ri, :], Xcar[:, ri, :], ALU.subtract)
                    P.stt("dve", Xcar[:, ri, :], Xold[:, ri, :], mr, Xcar[:, ri, :], ALU.mult, ALU.add)
            P.copy("dve", SbA, S)
            dbg("S0", S)
            dbg("X0", Xcar)


        A.off = b2_base
        sgB = A.alloc([TT], F32)
        xmb = [A.alloc([8, TT + 2], F32) for _ in range(2)]
        hfb = [A.alloc([8, TT + 2], BF16) for _ in range(2)]
        hid = A.alloc([22, TT], BF16)
        stg = [[A.alloc([TT + 2], F32) for _ in range(2)] for _ in range(2)]
        cgv = [A.alloc([2, TT], F32) for _ in range(2)]
        pb_ = A.alloc([2, TT], BF16)
        wpp = A.alloc([2, 1024], BF16)
        halo_a = A.alloc([44, 2], F32)
        wbufsB2 = [A.alloc([4096], BF16) for _ in range(5)]
        sq2 = A.alloc([8, 2], BF16)
        rs2 = A.alloc([2], F32)
        hp = hT
        B2ORDER = (1, 2, 3, 0)

        def b2_prep(idx):
            t = B2ORDER[idx]
            xm = xmb[idx % 2]
            hf = hfb[idx % 2]
            P.dma("sp", xm[:, :, 2:TT + 2], xmid_v[:, :, t * TT:(t + 1) * TT], "xm2_%d" % (idx % 2))
            if t == 1:
                P.dma("sp", xm[:, :, 0:2], xmid_v[:, :, t * TT - 2:t * TT], "xm2h")
            elif t == 0:
                hxv = hx.rearrange("p r (k c) -> p r k c", c=2)
                P.ts("dve", xm[:, :, 0:2], hxv[:, 0], meta[:, 4:5], None, ALU.mult)
                for r in range(1, 4):
                    P.stt("dve", xm[:, :, 0:2], hxv[:, r], meta[:, 4 + r:5 + r], xm[:, :, 0:2], ALU.mult, ALU.add)
            rmsnorm(xm[:, :, 2:TT + 2], gcol(P_GFFN), hf[:, :, 2:TT + 2], TT, sq, rs)
            if t in (1, 0):
                rmsnorm(xm[:, :, 0:2], gcol(P_GFFN), hf[:, :, 0:2], 2, sq2, rs2)

        for t in range(NT):
            if t > 0:
                b1_prefix(t)
            if t > 0:
                s5_chain()
            wqs = wq.next()
            for m in range(4):
                pq = bank()
                for k in range(8):
                    P.mm(pq, wqs[:, k, m * 128:(m + 1) * 128], hT[:, k, :], start=(k == 0), stop=(k == 7))
                P.act(qT[:, m, :], pq, AF.Silu)
            wog = wq.next()
            for m in range(4):
                pq = bank()
                for k in range(8):
                    P.mm(pq, wog[:, k, m * 128:(m + 1) * 128], hT[:, k, :], start=(k == 0), stop=(k == 7))
                P.act(ogT[:, m, :], pq, AF.Silu)
            if t == 0:
                combine_states()
                s5_chain()
            s5_out()
            wf = wq.next()
            wi = wq.next()
            hgrn_tile(wf, wi, True)
            if t == NT - 1:
                b2_prep(0)
            if t == 0:
                dbg("yhg", yhg)
                dbg("ys5", ys5)
                dbg("gf", gf)
            for half in range(2):
                wgh = wq.next()
                wgs = wq.next()
                for mm_ in range(4):
                    m = half * 4 + mm_
                    ms = slice(m * 128, (m + 1) * 128)
                    pa = bank()
                    for h in range(4):
                        P.mm(pa, wbh[:, h, ms], yhg[:, h, :], start=(h == 0), stop=(h == 3))
                    pb = bank()
                    for h in range(4):
                        P.mm(pb, wbs[:, h, ms], ys5[:, h, :], start=(h == 0), stop=(h == 3))
                    pc = bank()
                    for k in range(8):
                        P.mm(pc, wgh[:, k, mm_ * 128:(mm_ + 1) * 128], hT[:, k, :], start=(k == 0), stop=(k == 7))
                    pd = bank()
                    for k in range(8):
                        P.mm(pd, wgs[:, k, mm_ * 128:(mm_ + 1) * 128], hT[:, k, :], start=(k == 0), stop=(k == 7))
                    P.act(sg, pc, AF.Sigmoid)
                    P.act(sg2, pd, AF.Sigmoid)
                    P.tt("dve", mt1, sg, pa, ALU.mult)
                    P.tt("dve", sg2, sg2, pb, ALU.mult)
                    P.tt("dve", merged[:, m, :], mt1, sg2, ALU.add)
            load_x(t)
            for half in range(2):
                wo = wq.next()
                for mm_ in range(4):
                    m = half * 4 + mm_
                    po = bank()
                    for k in range(8):
                        P.mm(po, wo[:, k, mm_ * 128:(mm_ + 1) * 128], merged[:, k, :], start=(k == 0), stop=(k == 7))
                    P.tt("dve", xt[:, m, :], xt[:, m, :], po, ALU.add)
            for hh in range(2):
                P.dma("sp", xmid_v[:, 4 * hh:4 * hh + 4, t * TT:(t + 1) * TT], xt[:, 4 * hh:4 * hh + 4, :], "xm%d" % hh)
            if t == 0:
                dbg("xmid0", xt)
                dbg("merged0", merged)
            if t == NT - 1:
                P.dma("sp", hsrc_d.rearrange("p (k c) -> p k c", c=2), xt[:, :, TT - 2:TT], "hs")
                P.add("pool", lambda e: e.collective_compute("AllGather", ALU.bypass,
                                                              replica_groups=[[0, 1, 2, 3], [4, 5, 6, 7]],
                                                              ins=[hsrc_d], outs=[hdst_d]), reads=[hsrc_d], writes=[hdst_d],
                      is_dma=True, dma_slot="cc2", inc=1)
                P.dma("sp", hx, hdst_d.rearrange("(r p) f -> p r f", p=128), "hx")

        sg = sgB
        P.dma("pool", wpp, w_pp_v, "wpp")
        wq.set_bufs(wbufsB2)
        for idx, t in enumerate(B2ORDER):
            xm = xmb[idx % 2]
            hf = hfb[idx % 2]
            P.dma("pool", pb_, pT_v[:, :, t * TT:(t + 1) * TT], "pT")
            carry = (t in (2, 3))
            for j in range(22):
                if j % 4 == 0:
                    wg_ = wq.next()
                    wv_ = wq.next()
                sb_ = stg[j % 2]
                cb_ = cgv[j % 2]
                for gv in range(2):
                    jj = gv * 22 + j
                    pm = bank()
                    for k in range(8):
                        P.mm(pm, (wg_, wv_)[gv][:, k, (j % 4) * 128:(j % 4 + 1) * 128], hf[:, k, 2:TT + 2], start=(k == 0), stop=(k == 7))
                    if carry:
                        P.copy("act", sb_[gv][:, 0:2], halo_a[:, jj, :])
                    else:
                        ph = bank()
                        for k in range(8):
                            P.mm(ph[:, 0:2], (wg_, wv_)[gv][:, k, (j % 4) * 128:(j % 4 + 1) * 128], hf[:, k, 0:2], start=(k == 0), stop=(k == 7))
                        P.act(sb_[gv][:, 0:2], ph[:, 0:2], AF.Copy)
                    P.act(sb_[gv][:, 2:TT + 2], pm, AF.Copy)
                    P.act(cb_[:, gv, :], pm, AF.Identity, scale=convw[:, 2, jj:jj + 1], bias=convb[:, jj:jj + 1])
                    if t != 0:
                        P.copy("act", halo_a[:, jj, :], sb_[gv][:, TT:TT + 2])
                    P.stt("dve", cb_[:, gv, :], sb_[gv][:, 1:TT + 1], convw[:, 1, jj:jj + 1], cb_[:, gv, :], ALU.mult, ALU.add)
                    P.stt("dve", cb_[:, gv, :], sb_[gv][:, 0:TT], convw[:, 0, jj:jj + 1], cb_[:, gv, :], ALU.mult, ALU.add)
                P.act(cb_[:, 0, :], cb_[:, 0, :], AF.Gelu_apprx_tanh)
                P.tt("dve", hid[:, j, :], cb_[:, 0, :], cb_[:, 1, :], ALU.mult)
            if t == 1:
                dbg("hf1", hf)
                dbg("hid1", hid)
            if idx + 1 < NT:
                b2_prep(idx + 1)
            pos = [psb[m][:, :] for m in range(8)]
            for (j0, nj) in FGRP:
                wd = wq.next()
                for jl in range(nj):
                    j = j0 + jl
                    for m in range(8):
                        P.mm(pos[m], wd[:, jl, m * 128:(m + 1) * 128], hid[:, j, :], start=(j == 0), stop=(j == 21))
            for m in range(8):
                P.tt("dve", xm[:, m, 2:TT + 2], xm[:, m, 2:TT + 2], pos[m], ALU.add)
            if t == 1:
                dbg("x2_1", xm)
            rmsnorm(xm[:, :, 2:TT + 2], gcol(P_GPLE), hp, TT, sq, rs)
            for half in range(2):
                wpg = wq.next()
                for mm_ in range(4):
                    m = half * 4 + mm_
                    pg = bank()
                    for k in range(8):
                        P.mm(pg, wpg[:, k, mm_ * 128:(mm_ + 1) * 128], hp[:, k, :], start=(k == 0), stop=(k == 7))
                    pp = bank()
                    for k in range(2):
                        P.mm(pp, wpp[:, k, m * 128:(m + 1) * 128], pb_[:, k, :], start=(k == 0), stop=(k == 1))
                    P.act(sg, pg, AF.Sigmoid)
                    P.tt("dve", sg, sg, pp, ALU.mult)
                    P.tt("dve", xm[:, m, 2:TT + 2], xm[:, m, 2:TT + 2], sg, ALU.add)
            if t == 1:
                dbg("x3_1", xm)
            rmsnorm(xm[:, :, 2:TT + 2], gcol(P_GFIN), xt, TT, sq, rs)
            P.dma("sp", outT_v[:, :, t * TT:(t + 1) * TT], xt, "out")
        global _LASTP, _DBGAP
        _LASTP = P
        _DBGAP = dict(qT=qT, uT=uT, wre=wre)
        P.emit()
    return nc, dbg_out


def _consts():
    c = np.zeros((128, C_N), np.float32)
    s = np.arange(128)[:, None]
    t = np.arange(128)[None, :]
    same = (s // 64) == (t // 64)
    tri = (same & (s <= t)).astype(np.float32)
    ref = (same & ((s % 64) <= 32)).astype(np.float32)
    c[:, C_ID:C_ID + 128] = np.eye(128, dtype=np.float32)
    c[:, C_MQ:C_MQ + 128] = tri - ref
    c[:, C_TRI:C_TRI + 128] = tri
    c[:, C_IND:C_IND + 2] = (np.arange(128)[:, None] // 64 == np.arange(2)[None, :]).astype(np.float32)
    c[:, C_NMQ:C_NMQ + 128] = ref - tri
    c[:, C_ML:C_ML + 128] = same.astype(np.float32) - tri
    c[:, C_RM:C_RM + 4] = (np.arange(128)[:, None] // 32 == np.arange(4)[None, :]).astype(np.float32)
    c[:, C_ONES:C_ONES + 128] = 1.0
    return c


def _pair_layout(a):
    q = np.arange(16)
    ct, qp = q % 4, q // 4
    out = np.empty((2, 64, 16) + a.shape[2:], a.dtype)
    for g2 in range(2):
        g = 8 * ct + 2 * qp + g2
        out[g2] = np.moveaxis(a[g], 0, 1)
    return out.reshape((128, 16) + a.shape[2:])


_CACHE = {}


def _get_prog(debug=()):
    key = tuple(debug)
    if key not in _CACHE:
        _CACHE[key] = build(debug)
    return _CACHE[key]


def kernel(x, p, norm_mix_g, w_in, hg_lb_logits, hg_norm_g, s5_lambda_re, s5_lambda_im, s5_log_dt, s5_b_re, s5_b_im,
           s5_c_re, s5_c_im, s5_d, s5_glu_w, s5_glu_b, w_branch_hg, w_branch_s5, w_out, norm_ffn_g, w_up, conv_w, conv_b,
           w_down, norm_ple_g, w_ple_gate, w_ple_proj, norm_final_g, _debug=()):
    f = lambda a: np.ascontiguousarray(np.asarray(a, dtype=np.float32))
    x = f(x)
    p = f(p)
    nc, dbg_out = _get_prog(_debug)
    col8 = lambda g: f(g).reshape(8, 128).T
    prm = np.zeros((128, P_N), np.float32)
    prm[:, P_GMIX:P_GMIX + 8] = col8(norm_mix_g[0])
    prm[:, P_GFFN:P_GFFN + 8] = col8(norm_ffn_g[0])
    prm[:, P_GPLE:P_GPLE + 8] = col8(norm_ple_g[0])
    prm[:, P_GFIN:P_GFIN + 8] = col8(norm_final_g)
    prm[:, P_HGG] = f(hg_norm_g[0])
    prm[:, P_GLUB:P_GLUB + 4] = f(s5_glu_b[0]).reshape(4, 128).T
    prm[:, P_DSK:P_DSK + 4] = f(s5_d[0]).reshape(4, 128).T
    prm[:, P_CW:P_CW + 132] = f(conv_w[0]).reshape(3, 44, 128).transpose(2, 0, 1).reshape(128, 132)
    prm[:, P_CB:P_CB + 44] = f(conv_b[0]).reshape(44, 128).T
    prq = np.zeros((128, Q_N), np.float32)
    lb = f(hg_lb_logits)
    prq[:, Q_L0:Q_L0 + 512] = lb[0][None, :]
    prq[:, Q_L1:Q_L1 + 512] = lb[1][None, :]
    prq[:, Q_LRE:Q_LRE + 16] = _pair_layout(f(s5_lambda_re[0]))
    prq[:, Q_LIM:Q_LIM + 16] = _pair_layout(f(s5_lambda_im[0]))
    prq[:, Q_LDT:Q_LDT + 16] = _pair_layout(np.repeat(f(s5_log_dt[0])[:, None], 64, axis=1))
    prq[:, Q_BRE:Q_BRE + 256] = _pair_layout(f(s5_b_re[0])).reshape(128, 256)
    prq[:, Q_BIM:Q_BIM + 256] = _pair_layout(f(s5_b_im[0])).reshape(128, 256)
    prq[:, Q_CRE:Q_CRE + 256] = _pair_layout(f(s5_c_re[0]).transpose(0, 2, 1)).reshape(128, 256)
    prq[:, Q_CIM:Q_CIM + 256] = _pair_layout(f(s5_c_im[0]).transpose(0, 2, 1)).reshape(128, 256)
    cst = _consts()
    shared = {
        "cst": cst, "prq": prq, "w_in": f(w_in[0]), "w_bh": f(w_branch_hg[0]), "w_bs": f(w_branch_s5[0]),
        "w_glu": f(s5_glu_w[0]), "w_out": f(w_out[0]), "w_up": f(w_up[0]), "w_down": f(w_down[0]),
        "w_pg": f(w_ple_gate[0]), "w_pp": f(w_ple_proj[0]),
    }
    in_maps = []
    for c in range(NCORES):
        b, j = c // 4, c % 4
        pr = prm.copy()
        for r in range(4):
            pr[:, P_META + r] = 1.0 if r < j else 0.0
            pr[:, P_META + 4 + r] = 1.0 if r == j - 1 else 0.0
        d = dict(shared)
        d["prm"] = pr
        d["xT"] = np.ascontiguousarray(x[b, j * TOK:(j + 1) * TOK, :].T)
        d["pT"] = np.ascontiguousarray(p[0, b, j * TOK:(j + 1) * TOK, :].T)
        in_maps.append(d)
    res = run_bass_kernel_spmd(nc, in_maps, core_ids=list(range(NCORES)))
    out = np.empty((2, 8192, 1024), np.float32)
    for c in range(NCORES):
        b, j = c // 4, c % 4
        out[b, j * TOK:(j + 1) * TOK, :] = res.results[c]["outT"].T
    if _debug:
        kernel.last_debug = [{k: res.results[c]["dbg_" + k] for k in dbg_out} for c in range(NCORES)]
    return out
```

```python
import contextlib
import numpy as np
import concourse.bass as bass
import concourse.mybir as mybir
from concourse.bass_utils import run_bass_kernel_spmd

F32 = mybir.dt.float32
BF16 = mybir.dt.bfloat16
AF = mybir.ActivationFunctionType
ALU = mybir.AluOpType

NCORES = 8
TOK = 2048
TT = 512
NT = TOK // TT
EPS = 1e-6
HALF_PI = 1.5707963267948966

C_ID, C_MQ, C_TRI, C_IND, C_NMQ, C_ML, C_RM, C_ONES, C_N = 0, 128, 256, 384, 392, 520, 648, 656, 784
P_GMIX, P_GFFN, P_GPLE, P_GFIN, P_HGG, P_GLUB, P_DSK, P_CW, P_CB, P_META = 0, 8, 16, 24, 32, 33, 37, 41, 173, 217
P_N = 228
Q_L0, Q_L1, Q_LRE, Q_LIM, Q_LDT, Q_BRE, Q_BIM, Q_CRE, Q_CIM, Q_N = 0, 512, 1024, 1040, 1056, 1072, 1328, 1584, 1840, 2096
XC = 548

GRAN = 256
WAITLOG = []


class Op:
    __slots__ = ("eng", "fn", "deps", "signal", "sig_no", "dma_slot", "dma_val", "is_dma", "inc", "rk", "wk", "seq")


class Prog:
    ENGS = ("pe", "act", "dve", "pool", "sp")

    def __init__(self, nc):
        self.nc = nc
        self.ops = {e: [] for e in self.ENGS}
        self.all = []
        self.state = {}

    def _keys(self, ap):
        t = ap.tensor
        name = t.name
        sp = str(ap.space).upper()
        if not ("SB" in sp or "PSUM" in sp):
            return [(name, 0, 0)]
        F = 1
        for s in t.shape[1:]:
            F *= s
        esz = 4 if ap.dtype == F32 else 2
        tsz = 4 if t.dtype == F32 else 2
        Fe = F * tsz // esz
        off = ap.offset
        p0 = off // Fe
        f0 = off % Fe
        apl = ap.ap
        pstep, pcnt = apl[0]
        if pstep == 0:
            pcnt = 1
        ext = 0
        for st, cn in apl[1:]:
            ext += abs(st) * (cn - 1)
        lo = f0 * esz
        hi = (f0 + ext + 1) * esz
        ks = []
        is_psum = "PSUM" in sp
        for q in range(p0 // 32, (p0 + pcnt - 1) // 32 + 1):
            if is_psum:
                ks.append((name, q, 0))
                continue
            for g in range(lo // GRAN, (hi - 1) // GRAN + 1):
                ks.append((name, q, g))
        return ks

    def add(self, eng, fn, reads=(), writes=(), is_dma=False, dma_slot=None, inc=16):
        op = Op()
        op.inc = inc
        op.eng = eng
        op.fn = fn
        op.is_dma = is_dma
        op.signal = False
        op.deps = set()
        op.sig_no = None
        op.dma_slot = dma_slot
        op.dma_val = None
        op.seq = len(self.all)
        op.rk = []
        op.wk = []
        st = self.state
        writes = list(writes) + [a for a in reads if "PSUM" in str(a.space).upper()]
        reads = [a for a in reads if "PSUM" not in str(a.space).upper()]
        for a in reads:
            for k in self._keys(a):
                op.rk.append(k)
                s = st.get(k)
                if s is None:
                    s = st[k] = [None, []]
                if s[0] is not None:
                    op.deps.add(s[0])
                s[1].append(op)
        for a in writes:
            for k in self._keys(a):
                op.wk.append(k)
                s = st.get(k)
                if s is None:
                    s = st[k] = [None, []]
                if s[0] is not None:
                    op.deps.add(s[0])
                for r in s[1]:
                    op.deps.add(r)
                s[0] = op
                s[1] = []
        op.deps.discard(op)
        self.ops[eng].append(op)
        self.all.append(op)
        return op

    def emit(self):
        nc = self.nc
        for op in self.all:
            nd = set()
            for d in op.deps:
                if (not d.is_dma) and (not op.is_dma) and d.eng == op.eng == "pe":
                    continue
                nd.add(d)
            last = {}
            keep = set()
            for d in nd:
                if d.is_dma:
                    keep.add(d)
                elif d.eng not in last or d.seq > last[d.eng].seq:
                    last[d.eng] = d
            keep.update(last.values())
            op.deps = keep
            for d in keep:
                d.signal = True
        for e in self.ENGS:
            n = 0
            for op in self.ops[e]:
                if (not op.is_dma) and op.signal:
                    n += 1
                    op.sig_no = n
        slot_cnt = {}
        for op in self.all:
            if op.is_dma:
                c = slot_cnt.get(op.dma_slot, 0) + op.inc
                slot_cnt[op.dma_slot] = c
                op.dma_val = c
        with contextlib.ExitStack() as es:
            esem = {e: es.enter_context(nc.semaphore("sem_" + e)) for e in self.ENGS}
            dsem = {k: es.enter_context(nc.semaphore("dsem_%d" % i)) for i, k in enumerate(slot_cnt)}
            block = es.enter_context(nc.Block())

            def run(ename, eng):
                waited = {}
                for op in self.ops[ename]:
                    need = {}
                    for d in op.deps:
                        if d.is_dma:
                            key = ("d", d.dma_slot)
                            val = d.dma_val
                            sem = dsem[d.dma_slot]
                        else:
                            key = ("e", d.eng)
                            val = d.sig_no
                            sem = esem[d.eng]
                        if val > need.get(key, (0, None))[0]:
                            need[key] = (val, sem)
                    for key, (val, sem) in need.items():
                        if waited.get(key, 0) >= val:
                            continue
                        eng.wait_ge(sem, val)
                        waited[key] = val
                        WAITLOG.append((ename, op.seq, key, val))
                    if op.is_dma and op.dma_val - op.inc > 0:
                        key = ("d", op.dma_slot)
                        if waited.get(key, 0) < op.dma_val - op.inc:
                            eng.wait_ge(dsem[op.dma_slot], op.dma_val - op.inc)
                            waited[key] = op.dma_val - op.inc
                    ins = op.fn(eng)
                    if op.is_dma:
                        ins.then_inc(dsem[op.dma_slot], op.inc)
                    elif op.signal:
                        ins.then_inc(esem[ename], 1)
                if ename == "sp":
                    for k, c in slot_cnt.items():
                        eng.wait_ge(dsem[k], c)

            @block.tensor
            def _(eng):
                run("pe", eng)

            @block.scalar
            def _(eng):
                run("act", eng)

            @block.vector
            def _(eng):
                run("dve", eng)

            @block.gpsimd
            def _(eng):
                run("pool", eng)

            @block.sync
            def _(eng):
                run("sp", eng)

    def dma(self, eng, out, in_, slot, **kw):
        return self.add(eng, lambda e: e.dma_start(out=out, in_=in_, **kw), reads=[in_], writes=[out],
                        is_dma=True, dma_slot=slot)

    def mm(self, out, lhsT, rhs, start=True, stop=True, **kw):
        return self.add("pe", lambda e: e.matmul(out, lhsT, rhs, start=start, stop=stop, **kw),
                        reads=[lhsT, rhs], writes=[out])

    def act(self, out, in_, func, **kw):
        rd = [in_] + [v for v in kw.values() if hasattr(v, "tensor")]
        return self.add("act", lambda e: e.activation(out=out, in_=in_, func=func, **kw), reads=rd, writes=[out])

    def tt(self, eng, out, in0, in1, op):
        return self.add(eng, lambda e: e.tensor_tensor(out=out, in0=in0, in1=in1, op=op), reads=[in0, in1], writes=[out])

    def ts(self, eng, out, in0, s1, s2, op0, op1=None):
        rd = [in0] + [v for v in (s1, s2) if hasattr(v, "tensor")]
        if op1 is None:
            return self.add(eng, lambda e: e.tensor_scalar(out=out, in0=in0, scalar1=s1, scalar2=None, op0=op0),
                            reads=rd, writes=[out])
        return self.add(eng, lambda e: e.tensor_scalar(out=out, in0=in0, scalar1=s1, scalar2=s2, op0=op0, op1=op1),
                        reads=rd, writes=[out])

    def stt(self, eng, out, in0, scalar, in1, op0, op1):
        rd = [in0, in1] + ([scalar] if hasattr(scalar, "tensor") else [])
        return self.add(eng, lambda e: e.scalar_tensor_tensor(out=out, in0=in0, scalar=scalar, in1=in1, op0=op0, op1=op1),
                        reads=rd, writes=[out])

    def copy(self, eng, out, in_):
        if eng == "act":
            return self.act(out, in_, AF.Copy)
        return self.add(eng, lambda e: e.tensor_copy(out=out, in_=in_), reads=[in_], writes=[out])

    def memset(self, eng, out, val):
        return self.add(eng, lambda e: e.memset(out, val), reads=[], writes=[out])

    def recip(self, out, in_):
        return self.add("dve", lambda e: e.reciprocal(out=out, in_=in_), reads=[in_], writes=[out])


class Arena:
    def __init__(self, nc, es, nbytes):
        self.t = es.enter_context(nc.sbuf_tensor("arena", [128, nbytes // 2], BF16))
        self.off = 0
        self.cap = nbytes

    def alloc(self, shape, dtype):
        esz = 4 if dtype == F32 else 2
        n = int(np.prod(shape))
        nb = n * esz
        al = GRAN if nb >= GRAN else 64
        self.off = (self.off + al - 1) // al * al
        o = self.off
        self.off += nb
        assert self.off <= self.cap, ("SBUF arena overflow", self.off, self.cap)
        v = self.t[:, o // 2:(o + nb) // 2]
        if dtype == F32:
            v = v.bitcast(F32)
        if len(shape) > 1:
            names = "abcde"[:len(shape)]
            pat = "p (" + " ".join(names) + ") -> p " + " ".join(names)
            v = v.rearrange(pat, **{names[i]: shape[i] for i in range(len(shape))})
        return v


def build(debug=()):
    nc = bass.Bass("TRN2", target_bir_lowering=False)
    dbg_out = {}

    def din(name, shape):
        return nc.dram_tensor(name, list(shape), F32, kind="ExternalInput").ap()

    xT = din("xT", [1024, TOK])
    pT = din("pT", [256, TOK])
    cst_d = din("cst", [128, C_N])
    prm_d = din("prm", [128, P_N])
    prq_d = din("prq", [128, Q_N])
    w_in = din("w_in", [1024, 4608])
    w_bh = din("w_bh", [512, 1024])
    w_bs = din("w_bs", [512, 1024])
    w_glu = din("w_glu", [512, 512])
    w_out = din("w_out", [1024, 1024])
    w_up = din("w_up", [1024, 5632])
    w_down = din("w_down", [2816, 1024])
    w_pg = din("w_pg", [1024, 1024])
    w_pp = din("w_pp", [256, 1024])
    outT = nc.dram_tensor("outT", [1024, TOK], F32, kind="ExternalOutput").ap()
    xmid_d = nc.dram_tensor("xmid", [1024, TOK], F32, kind="Internal").ap()
    xsrc_d = nc.dram_tensor("xsrc", [128, XC], F32, kind="Internal").ap()
    xdst_d = nc.dram_tensor("xdst", [4 * 128, XC], F32, kind="Internal").ap()
    hsrc_d = nc.dram_tensor("hsrc", [128, 16], F32, kind="Internal").ap()
    hdst_d = nc.dram_tensor("hdst", [4 * 128, 16], F32, kind="Internal").ap()

    w_in_v = w_in.rearrange("(k p) n -> p k n", p=128)
    w_out_v = w_out.rearrange("(k p) n -> p k n", p=128)
    w_up_v = w_up.rearrange("(k p) n -> p k n", p=128)
    w_down_v = w_down.rearrange("(k p) n -> p k n", p=128)
    w_pg_v = w_pg.rearrange("(k p) n -> p k n", p=128)
    w_pp_v = w_pp.rearrange("(k p) n -> p k n", p=128)
    w_bh_v = w_bh.rearrange("(k p) n -> p k n", p=128)
    w_bs_v = w_bs.rearrange("(k p) n -> p k n", p=128)
    w_glu_v = w_glu.rearrange("(k p) n -> p k n", p=128)
    xT_v = xT.rearrange("(k p) t -> p k t", p=128)
    pT_v = pT.rearrange("(k p) t -> p k t", p=128)
    xmid_v = xmid_d.rearrange("(k p) t -> p k t", p=128)
    outT_v = outT.rearrange("(k p) t -> p k t", p=128)

    with contextlib.ExitStack() as es:
        A = Arena(nc, es, 207 * 1024)
        psb = [es.enter_context(nc.psum_tensor("ps%d" % i, [128, 512], F32)) for i in range(8)]
        P = Prog(nc)
        bank_ctr = [0]

        def bank():
            b = psb[bank_ctr[0] % 6]
            bank_ctr[0] += 1
            return b[:, :]

        def dbg(name, ap):
            if name not in debug:
                return
            shp = [int(s) for s in ap.shape]
            d = nc.dram_tensor("dbg_" + name, shp, ap.dtype, kind="ExternalOutput").ap()
            P.dma("sp", d, ap, "dbg")
            dbg_out[name] = shp

        cst = A.alloc([C_N], F32)
        prm = A.alloc([P_N], F32)
        ident_f = cst[:, C_ID:C_ID + 128]
        CM_f = cst[:, C_MQ:C_MQ + 256]
        TRI_f = cst[:, C_TRI:C_TRI + 128]
        IND_f = cst[:, C_IND:C_IND + 2]
        NMQ_f = cst[:, C_NMQ:C_NMQ + 128]
        ML_f = cst[:, C_ML:C_ML + 128]
        RM_f = cst[:, C_RM:C_RM + 4]
        ident_b = A.alloc([128], BF16)
        ones_b = A.alloc([128], BF16)
        wbufs = [A.alloc([4096], BF16) for _ in range(3)]
        xt_off = (A.off + GRAN - 1) // GRAN * GRAN
        xt = A.alloc([8, TT], F32)
        xt_end = A.off
        A.off = xt_off
        gf = A.alloc([4, TT], F32)
        gb = A.alloc([4, TT], BF16)
        Xpb = A.alloc([16, 2, 64], BF16)
        assert A.off <= xt_end
        A.off = xt_end
        sq = A.alloc([8, TT], BF16)
        hT = A.alloc([8, TT], BF16)
        rs = A.alloc([TT], F32)
        b2_base = A.off
        LB = A.alloc([512], F32)
        OML = A.alloc([512], F32)
        S = A.alloc([4, 128], F32)
        SbA = A.alloc([4, 128], BF16)
        SbB = A.alloc([4, 128], BF16)
        logD = A.alloc([4], F32)
        dec = A.alloc([4, 2], F32)
        BD = A.alloc([4, 8, 128], BF16)
        WV = A.alloc([4, 8, 2, 128], BF16)
        WZ = A.alloc([16, 8, 2, 32], BF16)
        Ec = A.alloc([16, 64], F32)
        Es = A.alloc([16, 64], F32)
        Rtab = A.alloc([16, 64], F32)
        A8 = A.alloc([2, 16], F32)
        A2048 = A.alloc([2, 16], F32)
        Xcar = A.alloc([2, 16], F32)
        wbh = A.alloc([4, 1024], BF16)
        wbs = A.alloc([4, 1024], BF16)
        gluw = A.alloc([4, 512], BF16)
        mark = A.off

        gcol = lambda o: prm[:, o:o + 8]
        hgg = prm[:, P_HGG:P_HGG + 1]
        glub = prm[:, P_GLUB:P_GLUB + 4]
        dsk = prm[:, P_DSK:P_DSK + 4]
        convw = prm[:, P_CW:P_CW + 132].rearrange("p (k j) -> p k j", k=3)
        convb = prm[:, P_CB:P_CB + 44]
        meta = prm[:, P_META:P_META + 8]

        P.dma("sp", cst, cst_d, "c0")
        P.dma("sp", prm, prm_d, "c0")
        P.copy("dve", ident_b, ident_f)
        P.copy("dve", ones_b, cst[:, C_ONES:C_ONES + 128])
        P.dma("pool", wbh, w_bh_v, "wres")
        P.dma("pool", wbs, w_bs_v, "wres")
        P.dma("pool", gluw, w_glu_v, "wres")

        class WQ:
            def __init__(self):
                self.specs = []
                self.views = {}
                self.i = 0
                self.issued = 0
                self.bufs = wbufs
                self.gen = 0

            def set_bufs(self, bufs):
                self.bufs = bufs
                self.gen += 1

            def plan(self, parts, shape):
                self.specs.append((parts, shape))

            def _issue(self, k):
                parts, shape = self.specs[k]
                nb = len(self.bufs)
                buf = self.bufs[k % nb]
                n = int(np.prod(shape))
                v = buf[:, :n]
                names = "abcd"[:len(shape)]
                v = v.rearrange("p (" + " ".join(names) + ") -> p " + " ".join(names),
                                **{names[i]: shape[i] for i in range(len(shape))})
                for pi_, (sel, src) in enumerate(parts):
                    P.dma("pool", sel(v), src, "w%d_%d_%d" % (self.gen, k % nb, pi_))
                self.views[k] = v

            def next(self):
                while self.issued < min(len(self.specs), self.i + len(self.bufs) - 1):
                    self._issue(self.issued)
                    self.issued += 1
                v = self.views.pop(self.i)
                self.i += 1
                return v

        wq = WQ()
        FGRP = [(0, 4), (4, 4), (8, 4), (12, 4), (16, 4), (20, 2)]
        whole = lambda v: v
        for t in range(NT):
            for c0 in (512, 1024, 2048):
                wq.plan([(whole, w_in_v[:, :, c0:c0 + 512])], [8, 512])
        for t in range(NT):
            for c0 in (2048, 0, 1536, 512, 1024, 2560, 3584, 3072, 4096):
                wq.plan([(whole, w_in_v[:, :, c0:c0 + 512])], [8, 512])
            for c0 in (0, 512):
                wq.plan([(whole, w_out_v[:, :, c0:c0 + 512])], [8, 512])
        for t in range(NT):
            for (j0, nj) in FGRP:
                wq.plan([(whole, w_up_v[:, :, j0 * 128:(j0 + nj) * 128])], [8, nj * 128])
                wq.plan([(whole, w_up_v[:, :, 2816 + j0 * 128:2816 + (j0 + nj) * 128])], [8, nj * 128])
            for (j0, nj) in FGRP:
                wq.plan([(whole, w_down_v[:, j0:j0 + nj, :])], [nj, 1024])
            for c0 in (0, 512):
                wq.plan([(whole, w_pg_v[:, :, c0:c0 + 512])], [8, 512])

        def rmsnorm(x, g8, out, n, sq, rs):
            ss = bank()[:, :n]
            for k in range(8):
                if n >= 64:
                    P.act(sq[:, k, :], x[:, k, :], AF.Square)
                elif k == 0:
                    P.act(sq, x, AF.Square)
                P.mm(ss, ones_b, sq[:, k, :], start=(k == 0), stop=(k == 7))
            P.act(rs, ss, AF.Ln, scale=1.0 / 1024, bias=EPS)
            P.act(rs, rs, AF.Exp, scale=-0.5)
            for k in range(8):
                P.stt("dve", out[:, k, :], x[:, k, :], g8[:, k:k + 1], rs, ALU.mult, ALU.mult)

        def cmul(eng, outr, outi, ar, ai, br, bi, t1, t2):
            P.tt(eng, t1, ar, br, ALU.mult)
            P.tt(eng, t2, ai, bi, ALU.mult)
            P.tt(eng, outr, t1, t2, ALU.subtract)
            P.tt(eng, t1, ar, bi, ALU.mult)
            P.tt(eng, t2, ai, br, ALU.mult)
            P.tt(eng, outi, t1, t2, ALU.add)

        def csq(eng, r, i, t1, t2):
            P.tt(eng, t1, r, r, ALU.mult)
            P.tt(eng, t2, i, i, ALU.mult)
            P.stt(eng, i, r, 2.0, i, ALU.mult, ALU.mult)
            P.tt(eng, r, t1, t2, ALU.subtract)

        A.off = mark
        prq = A.alloc([Q_N], F32)
        P.dma("sp", prq, prq_d, "c1")
        P.tt("dve", LB, prq[:, Q_L0:Q_L0 + 512], prq[:, Q_L1:Q_L1 + 512], ALU.subtract)
        P.act(LB, LB, AF.Sigmoid)
        P.ts("dve", OML, LB, -1.0, 1.0, ALU.mult, ALU.add)
        lre = prq[:, Q_LRE:Q_LRE + 16]
        lim = prq[:, Q_LIM:Q_LIM + 16]
        ldt = prq[:, Q_LDT:Q_LDT + 16]
        Bre = prq[:, Q_BRE:Q_BRE + 256].rearrange("p (q c) -> p q c", q=16)
        Bim = prq[:, Q_BIM:Q_BIM + 256].rearrange("p (q c) -> p q c", q=16)
        Cre = prq[:, Q_CRE:Q_CRE + 256].rearrange("p (q c) -> p q c", q=16)
        Cim = prq[:, Q_CIM:Q_CIM + 256].rearrange("p (q c) -> p q c", q=16)
        sm = A.alloc([24, 16], F32)
        dt_, lrdt, lidt, mag, ur, ui, t1, t2, are, aim, mag8, u8r, u8i, cr, ci, den, xx, pkr, pki = [sm[:, i, :] for i in range(19)]
        PW = A.alloc([9, 2, 16], F32)
        P.act(dt_, ldt, AF.Exp)
        P.tt("dve", lrdt, lre, dt_, ALU.mult)
        P.tt("dve", lidt, lim, dt_, ALU.mult)
        P.act(mag, lrdt, AF.Exp, scale=1.0 / 256)
        P.act(ui, lidt, AF.Sin, scale=1.0 / 256)
        P.act(ur, lidt, AF.Sin, scale=1.0 / 256, bias=HALF_PI)
        for _ in range(8):
            P.tt("dve", mag, mag, mag, ALU.mult)
            csq("dve", ur, ui, t1, t2)
        P.tt("dve", are, mag, ur, ALU.mult)
        P.tt("dve", aim, mag, ui, ALU.mult)
        P.copy("dve", mag8, mag)
        P.copy("dve", u8r, ur)
        P.copy("dve", u8i, ui)
        for _ in range(3):
            P.tt("dve", mag8, mag8, mag8, ALU.mult)
            csq("dve", u8r, u8i, t1, t2)
        P.memset("dve", PW[:, 0, 0, :], 1.0)
        P.memset("dve", PW[:, 0, 1, :], 0.0)
        P.copy("dve", PW[:, 1, 0, :], are)
        P.copy("dve", PW[:, 1, 1, :], aim)
        for m in range(2, 9):
            cmul("dve", PW[:, m, 0, :], PW[:, m, 1, :], PW[:, m - 1, 0, :], PW[:, m - 1, 1, :], are, aim, t1, t2)
        P.copy("dve", A8[:, 0, :], PW[:, 8, 0, :])
        P.copy("dve", A8[:, 1, :], PW[:, 8, 1, :])
        P.copy("dve", A2048[:, 0, :], PW[:, 8, 0, :])
        P.copy("dve", A2048[:, 1, :], PW[:, 8, 1, :])
        for _ in range(8):
            csq("dve", A2048[:, 0, :], A2048[:, 1, :], t1, t2)
        P.tt("dve", den, lre, lre, ALU.mult)
        P.tt("dve", t1, lim, lim, ALU.mult)
        P.tt("dve", den, den, t1, ALU.add)
        P.recip(den, den)
        P.ts("dve", xx, are, -1.0, None, ALU.add)
        P.tt("dve", t1, xx, lre, ALU.mult)
        P.tt("dve", t2, aim, lim, ALU.mult)
        P.tt("dve", cr, t1, t2, ALU.add)
        P.tt("dve", cr, cr, den, ALU.mult)
        P.tt("dve", t1, aim, lre, ALU.mult)
        P.tt("dve", t2, xx, lim, ALU.mult)
        P.tt("dve", ci, t1, t2, ALU.subtract)
        P.tt("dve", ci, ci, den, ALU.mult)
        big = A.alloc([8, 16, 16], F32)
        bbr, bbi, T1, T2, fr, fi = [big[:, i] for i in range(6)]
        bc = lambda v: v.rearrange("p (q o) -> p q o", o=1).broadcast_to([128, 16, 16])
        cmul("dve", bbr, bbi, bc(cr), bc(ci), Bre, Bim, T1, T2)
        P.memset("dve", Ec[:, :, 0:1], 1.0)
        P.memset("dve", Es[:, :, 0:1], 0.0)
        P.copy("dve", pkr, u8r)
        P.copy("dve", pki, u8i)
        Tt = A.alloc([2, 16, 32], F32)
        for k in range(6):
            n = 1 << k
            bq = lambda v: v.rearrange("p (q o) -> p q o", o=1).broadcast_to([128, 16, n])
            cmul("dve", Ec[:, :, n:2 * n], Es[:, :, n:2 * n], Ec[:, :, 0:n], Es[:, :, 0:n], bq(pkr), bq(pki),
                 Tt[:, 0, :, 0:n], Tt[:, 1, :, 0:n])
            if k < 5:
                csq("dve", pkr, pki, t1, t2)
        P.copy("dve", Rtab, mag8.rearrange("p (q o) -> p q o", o=1).broadcast_to([128, 16, 64]))
        P.memset("dve", Rtab[:, :, 0:1], 0.0)
        WVs = A.alloc([2, 16, 32], F32)
        WVs7 = A.alloc([2, 16, 32], F32)
        Zbd = A.alloc([2, 16, 32], F32)
        BDf = A.alloc([8, 128], F32)
        P.memset("pool", WVs, 0.0)
        P.memset("pool", WVs7, 0.0)
        P.memset("pool", Zbd, 0.0)
        pwb = lambda m, ri: PW[:, m, ri, :].rearrange("p (q o) -> p q o", o=1).broadcast_to([128, 16, 16])

        def fill_bd(dst, vr, vi, neg_im, ctmajor=False):
            for h in range(2):
                ps_ = slice(64 * h, 64 * h + 64)
                cs_ = slice(16 * h, 16 * h + 16)
                for ri, v in ((0, vr), (1, vi)):
                    d_ = dst[ps_, ri, :, cs_]
                    s_ = v[ps_]
                    if ctmajor:
                        d_ = d_.rearrange("p (ct qp) c -> p qp ct c", ct=4)
                        s_ = s_.rearrange("p (qp ct) c -> p qp ct c", ct=4)
                    if ri == 1 and neg_im:
                        P.ts("dve", d_, s_, -1.0, None, ALU.mult)
                    else:
                        P.copy("dve", d_, s_)

        for i in (7, 6, 5, 4, 3, 2, 1, 0):
            dst = WVs7 if i == 7 else WVs
            cmul("dve", fr, fi, pwb(7 - i, 0), pwb(7 - i, 1), bbr, bbi, T1, T2)
            fill_bd(dst, fr, fi, False, ctmajor=True)
            pb0 = bank()
            pb1 = bank()
            for ct in range(4):
                for ri in range(2):
                    pb = pb0 if ct < 2 else pb1
                    o = ((ct % 2) * 2 + ri) * 128
                    src = dst[:, ri, 4 * ct:4 * ct + 4, :].rearrange("p q c -> p (q c)")
                    P.mm(pb[:, o:o + 128], src, ident_f)
            for hh, pb in enumerate((pb0, pb1)):
                P.copy("act", WV[:, 2 * hh:2 * hh + 2, i, :, :], pb.rearrange("p (c r n) -> p c r n", c=2, r=2))
        pBD = [bank() for _ in range(4)]
        for m in range(9):
            cmul("dve", fr, fi, pwb(m, 0), pwb(m, 1), Cre, Cim, T1, T2)
            fill_bd(Zbd, fr, fi, True)
            if m >= 1:
                P.copy("act", WZ[:, :, m - 1, :, :], Zbd.rearrange("p r q c -> p q r c"))
            if m <= 7:
                for ct in range(4):
                    for qp in range(4):
                        q = qp * 4 + ct
                        for ri in range(2):
                            P.mm(pBD[ct][32 * qp:32 * qp + 32, m * 32:(m + 1) * 32], WVs7[:, ri, ct * 4 + qp, :], Zbd[:, ri, q, :],
                                 start=(ri == 0), stop=(ri == 1), tile_position=(0, 32 * qp))
        for ct in range(4):
            pv = pBD[ct][:, 0:256].rearrange("p (m c) -> p m c", m=8)
            for qp in range(4):
                P.ts("dve", BDf[:, :, 32 * qp:32 * qp + 32], pv, RM_f[:, qp:qp + 1], None, ALU.mult)
            P.stt("dve", BDf[:, 0, :], ident_f, dsk[:, ct:ct + 1], BDf[:, 0, :], ALU.mult, ALU.add)
            P.copy("act", BD[:, ct], BDf)
        dbg("BD", BD)
        dbg("WV", WV)
        dbg("WZ", WZ)
        dbg("Ec", Ec)
        dbg("PW", PW)

        A.off = mark
        qT = A.alloc([4, TT], BF16)
        ogT = A.alloc([4, TT], BF16)
        uT = A.alloc([4, TT], BF16)
        yhg = A.alloc([4, TT], BF16)
        ys5 = A.alloc([4, TT], BF16)
        ov0 = A.off
        sig = A.alloc([512], F32)
        lf = A.alloc([512], F32)
        kk = A.alloc([512], F32)
        eT = A.alloc([512], F32)
        khat = A.alloc([512], BF16)
        ktil = A.alloc([512], BF16)
        vb = A.alloc([512], BF16)
        alt0 = A.off
        ebq = A.alloc([4, 256], F32)
        qtil = A.alloc([4, 128], BF16)
        qhat = A.alloc([4, 128], BF16)
        ktT = A.alloc([4, 128], BF16)
        scT = A.alloc([4, 128], BF16)
        osq = A.alloc([512], BF16)
        orstd = A.alloc([512], F32)
        oy = A.alloc([512], F32)
        ov1 = A.off
        A.off = alt0
        SET0 = (sig, lf, kk, eT, khat, vb)
        SET1 = (A.alloc([512], F32), A.alloc([512], F32), A.alloc([512], F32), A.alloc([512], F32),
                A.alloc([512], BF16), A.alloc([512], BF16))
        assert A.off <= ov1
        A.off = ov0
        wre = A.alloc([16, 64], F32)
        wim = A.alloc([16, 64], F32)
        tA = A.alloc([16, 64], F32)
        tB = A.alloc([16, 64], F32)
        zre = A.alloc([16, 64], F32)
        zim = A.alloc([16, 64], F32)
        cc = A.alloc([4, 16], F32)
        Xold = A.alloc([2, 16], F32)
        A.off = max(A.off, ov1)
        A_sgz_raw = A.alloc([2, TT], F32)
        sg = A_sgz_raw[:, 0, :]
        sg2 = A_sgz_raw[:, 1, :]
        A_sgz = A_sgz_raw.rearrange("p a b -> p (a b)").rearrange("p (q c) -> p q c", q=16)
        mt1 = A.alloc([TT], F32)
        merged = sq
        xg = A.alloc([4, XC], F32)
        hx = A.alloc([4, 16], F32)
        vb_alt = A.alloc([512], BF16)
        b1_end = A.off

        P.memset("dve", S, 0.0)
        P.memset("dve", logD, 0.0)
        P.memset("dve", Xcar, 0.0)

        def hgrn_front(wf, wi, bi, full):
            tok = slice(bi * 128, (bi + 1) * 128)
            sig, lf, kk, eT, khat, vb = SET0 if (full or bi % 2 == 0) else SET1
            if full and bi % 2 == 1:
                vb = vb_alt
            pf = bank()
            for k in range(8):
                P.mm(pf, hT[:, k, tok], wf[:, k, :], start=(k == 0), stop=(k == 7))
            pi_ = bank()
            for k in range(8):
                P.mm(pi_, hT[:, k, tok], wi[:, k, :], start=(k == 0), stop=(k == 7))
            P.act(sig, pf, AF.Sigmoid)
            P.tt("dve", sig, sig, OML, ALU.mult)
            P.tt("dve", sig, sig, LB, ALU.add)
            P.act(lf, sig, AF.Ln)
            P.act(kk, sig, AF.Identity, scale=-1.0, bias=1.0)
            P.act(vb, pi_, AF.Copy)
            return dict(tok=tok, bufs=(sig, lf, kk, eT, khat, vb), full=full)

        def hgrn_mid(c):
            tok, full = c["tok"], c["full"]
            sig, lf, kk, eT, khat, vb = c["bufs"]
            pL = bank()
            P.mm(pL, ML_f, lf)
            if full:
                pC = [bank(), bank()]
                for h in range(4):
                    P.mm(pC[h // 2][:, (h % 2) * 256:(h % 2 + 1) * 256], lf[:, h * 128:(h + 1) * 128], CM_f)
                pK = bank()
                P.mm(pK, NMQ_f, lf)
            pD = bank()
            pDv = pD[:, 0:8].rearrange("p (h j) -> p h j", h=4)
            for h in range(4):
                P.mm(pDv[:, h, :], lf[:, h * 128:(h + 1) * 128], IND_f)
            P.act(eT, pL, AF.Exp)
            P.tt("dve", khat, kk, eT, ALU.mult)
            pU = [psb[6][:, :], psb[7][:, :]]
            for j in range(2):
                for h in range(4):
                    hs = slice(h * 128, (h + 1) * 128)
                    P.mm(pU[j][:, hs], khat[64 * j:64 * j + 64, hs], vb[64 * j:64 * j + 64, hs])
            c["pUv"] = [p_.rearrange("p (h e) -> p h e", h=4) for p_ in pU]
            if full:
                P.act(eT, pK, AF.Exp)
                P.tt("dve", ktil, kk, eT, ALU.mult)
                pTr = bank()
                for h in range(4):
                    hs = slice(h * 128, (h + 1) * 128)
                    P.mm(pTr[:, hs], ktil[:, hs], ident_b)
                for hh in range(2):
                    P.act(ebq[:, 2 * hh:2 * hh + 2, :], pC[hh].rearrange("p (h c) -> p h c", h=2), AF.Exp)
                P.tt("dve", qtil, qT[:, :, tok], ebq[:, :, 0:128], ALU.mult)
                P.copy("act", ktT, pTr.rearrange("p (h c) -> p h c", h=4))
                pS = bank()
                for h in range(4):
                    P.mm(pS[:, h * 128:(h + 1) * 128], ktT[:, h, :], qtil[:, h, :])
                P.tt("dve", qhat, qT[:, :, tok], ebq[:, :, 128:256], ALU.mult)
                P.tt("dve", scT, pS.rearrange("p (h c) -> p h c", h=4),
                     TRI_f.rearrange("p (o c) -> p o c", o=1).broadcast_to([128, 4, 128]), ALU.mult)
            P.act(dec, pDv, AF.Exp)
            if not full:
                P.tt("dve", logD, logD, pDv[:, :, 0], ALU.add)
                P.tt("dve", logD, logD, pDv[:, :, 1], ALU.add)

        def hgrn_back(c):
            tok, full = c["tok"], c["full"]
            vb = c["bufs"][5]
            pUv = c["pUv"]
            decb = lambda j: dec[:, :, j:j + 1].broadcast_to([128, 4, 128])
            P.tt("dve", S, S, decb(0), ALU.mult)
            if full:
                P.tt("dve", SbB, S, pUv[0], ALU.add)
            P.tt("dve", S, S, pUv[0], ALU.add)
            if full:
                pO = bank()
                for h in range(4):
                    hs = slice(h * 128, (h + 1) * 128)
                    P.mm(pO[:, hs], vb[:, hs], scT[:, h, :], start=True, stop=False)
                    P.mm(pO[:, h * 128:h * 128 + 64], SbA[:, h, :], qhat[:, h, 0:64], start=False, stop=False)
                    P.mm(pO[:, h * 128 + 64:h * 128 + 128], SbB[:, h, :], qhat[:, h, 64:128], start=False, stop=True)
            P.tt("dve", S, S, decb(1), ALU.mult)
            if full:
                P.tt("dve", SbA, S, pUv[1], ALU.add)
            P.tt("dve", S, S, pUv[1], ALU.add)
            if full:
                P.act(osq, pO, AF.Square)
                P.stt("dve", oy.rearrange("p (h c) -> p h c", h=4), pO.rearrange("p (h c) -> p h c", h=4), hgg, ogT[:, :, tok],
                      ALU.mult, ALU.mult)
                pN = bank()
                P.mm(pN, ones_b, osq)
                P.act(orstd, pN, AF.Ln, scale=1.0 / 128, bias=EPS)
                P.act(orstd, orstd, AF.Exp, scale=-0.5)
                P.tt("dve", yhg[:, :, tok], oy.rearrange("p (h c) -> p h c", h=4),
                     orstd.rearrange("p (h c) -> p h c", h=4), ALU.mult)

        def hgrn_tile(wf, wi, full, hooks=None):
            c = hgrn_front(wf, wi, 0, full)
            for bi in range(4):
                hgrn_mid(c)
                nxt = hgrn_front(wf, wi, bi + 1, full) if bi < 3 else None
                hgrn_back(c)
                if hooks and bi in hooks:
                    hooks[bi]()
                c = nxt

        def s5_stages(U, T):
            wre, wim, tA, tB, zre, zim, cc, Xold = T
            uv = U.rearrange("p t (c i) -> p t i c", i=8)
            st = {}

            def stage_v():
                P.copy("dve", Xold, Xcar)
                cmul("dve", cc[:, 0, :], cc[:, 1, :], A8[:, 0, :], A8[:, 1, :], Xold[:, 0, :], Xold[:, 1, :], cc[:, 2, :], cc[:, 3, :])
                pV = [bank() for _ in range(4)]
                st["pVv"] = [p_.rearrange("p (t r c) -> p t r c", t=4, r=2) for p_ in pV]
                for ct in range(4):
                    for ri in range(2):
                        for i in range(8):
                            for qp in range(4):
                                rsl = slice(32 * qp, 32 * qp + 32)
                                P.mm(st["pVv"][qp][:, ct, ri, :], WV[rsl, ct, i, ri, :], uv[rsl, ct, i, :],
                                     start=(i == 0), stop=(i == 7), tile_position=(32 * qp, 0))

            def stage_rotin():
                pVv = st["pVv"]
                for qp in range(4):
                    sl = slice(4 * qp, 4 * qp + 4)
                    vr = pVv[qp][:, :, 0, :]
                    vi = pVv[qp][:, :, 1, :]
                    P.tt("dve", wre[:, sl, :], vr, Ec[:, sl, :], ALU.mult)
                    P.tt("dve", tA[:, sl, :], vi, Es[:, sl, :], ALU.mult)
                    P.tt("dve", wim[:, sl, :], vi, Ec[:, sl, :], ALU.mult)
                    P.tt("dve", tB[:, sl, :], vr, Es[:, sl, :], ALU.mult)
                P.tt("dve", wre, wre, tA, ALU.add)
                P.tt("dve", wim, wim, tB, ALU.subtract)

            def stage_scan():
                P.tt("dve", wre[:, :, 0], wre[:, :, 0], cc[:, 0, :], ALU.add)
                P.tt("dve", wim[:, :, 0], wim[:, :, 0], cc[:, 1, :], ALU.add)
                fl = lambda v: v.rearrange("p q c -> p (q c)")

                def scan(z_, w_):
                    P.add("dve", lambda e: e.tensor_tensor_scan(out=fl(z_), data0=fl(Rtab), data1=fl(w_), initial=0.0,
                                                                 op0=ALU.mult, op1=ALU.add), reads=[Rtab, w_], writes=[z_])
                scan(zre, wre)
                scan(zim, wim)

            def stage_rotout():
                P.tt("dve", wre, zre, Ec, ALU.mult)
                P.tt("dve", tA, zim, Es, ALU.mult)
                P.tt("dve", wim, zim, Ec, ALU.mult)
                P.tt("dve", tB, zre, Es, ALU.mult)
                P.tt("dve", wre, wre, tA, ALU.subtract)
                P.tt("dve", wim, wim, tB, ALU.add)
                P.copy("dve", Xcar[:, 0, :], wre[:, :, 63])
                P.copy("dve", Xcar[:, 1, :], wim[:, :, 63])

            return stage_v, stage_rotin, stage_scan, stage_rotout

        TB = (wre, wim, tA, tB, zre, zim, cc, Xold)

        def s5_chain():
            for f_ in s5_stages(uT, TB):
                f_()
            P.copy("act", Xpb[:, :, 0, 1:64], wre[:, :, 0:63])
            P.copy("act", Xpb[:, :, 1, 1:64], wim[:, :, 0:63])
            P.copy("dve", Xpb[:, :, 0, 0], Xold[:, 0, :])
            P.copy("dve", Xpb[:, :, 1, 0], Xold[:, 1, :])

        def s5_out():
            uv = uT.rearrange("p t (c i) -> p t i c", i=8)
            for ct in range(4):
                pY = bank()
                pYv = pY.rearrange("p (i c) -> p i c", i=8)
                for tau in range(8):
                    P.mm(pYv[:, tau:8, :], BD[:, ct, tau, :], uv[:, ct, 0:8 - tau, :], start=(tau == 0), stop=False)
                for i in range(8):
                    for ri in range(2):
                        for qp in range(4):
                            q = qp * 4 + ct
                            P.mm(pYv[32 * qp:32 * qp + 32, i, :], WZ[:, q, i, ri, :], Xpb[:, q, ri, :],
                                 start=False, stop=(i == 7 and ri == 1), tile_position=(0, 32 * qp))
                P.act(gf[:, ct, :].rearrange("p (c i) -> p i c", i=8), pYv, AF.Gelu_apprx_tanh)
                P.copy("dve", gb[:, ct, :], gf[:, ct, :])
            for m in range(4):
                pG = bank()
                for ct in range(4):
                    P.mm(pG, gluw[:, ct, m * 128:(m + 1) * 128], gb[:, ct, :], start=(ct == 0), stop=(ct == 3))
                P.act(sg, pG, AF.Sigmoid, bias=glub[:, m:m + 1])
                P.tt("dve", ys5[:, m, :], gf[:, m, :], sg, ALU.mult)

        def load_x(t):
            for hh in range(2):
                P.dma("sp", xt[:, 4 * hh:4 * hh + 4, :], xT_v[:, 4 * hh:4 * hh + 4, t * TT:(t + 1) * TT], "x%d" % hh)

        f32v = lambda v, shp: v.rearrange("p a b -> p (a b)").bitcast(F32).rearrange("p (q c) -> p q c", q=shp[0])
        xgf = xg.rearrange("p a b -> p (a b)")
        TA = (f32v(qT, [16, 64]), f32v(ogT, [16, 64]), f32v(yhg, [16, 64]),
              xgf[:, 0:1024].rearrange("p (q c) -> p q c", q=16), xgf[:, 1024:2048].rearrange("p (q c) -> p q c", q=16),
              A_sgz, mt1[:, 0:64].rearrange("p (a b) -> p a b", a=4), mt1[:, 64:96].rearrange("p (a b) -> p a b", a=2))
        uTA = [uT, ys5]
        pend = None
        for t in range(NT):
            load_x(t)
            rmsnorm(xt, gcol(P_GMIX), hT, TT, sq, rs)
            wf = wq.next()
            wi = wq.next()
            hooks = None
            if pend:
                pend[0]()
                pend[1]()
                hooks = {0: pend[2], 1: pend[3]}
            hgrn_tile(wf, wi, False, hooks)
            wu = wq.next()
            U = uTA[t % 2]
            for m in range(4):
                pu = bank()
                for k in range(8):
                    P.mm(pu, wu[:, k, m * 128:(m + 1) * 128], hT[:, k, :], start=(k == 0), stop=(k == 7))
                P.copy("act", U[:, m, :], pu)
            pend = s5_stages(U, TA)
        def b1_prefix(t):
            load_x(t)
            rmsnorm(xt, gcol(P_GMIX), hT, TT, sq, rs)
            wu = wq.next()
            for m in range(4):
                pu = bank()
                for k in range(8):
                    P.mm(pu, wu[:, k, m * 128:(m + 1) * 128], hT[:, k, :], start=(k == 0), stop=(k == 7))
                P.copy("act", uT[:, m, :], pu)

        b1_prefix(0)
        for f_ in pend:
            f_()

        xs = gf[:, 0:2, :].rearrange("p a b -> p (a b)")[:, 0:XC]
        P.copy("dve", xs[:, 0:512], S.rearrange("p h e -> p (h e)"))
        P.copy("dve", xs[:, 512:516], logD)
        P.copy("dve", xs[:, 516:548], Xcar.rearrange("p r q -> p (r q)"))
        dbg("xs", xs)
        P.dma("sp", xsrc_d, xs, "xsrc")
        P.add("pool", lambda e: e.collective_compute("AllGather", ALU.bypass, replica_groups=[[0, 1, 2, 3], [4, 5, 6, 7]],
                                                      ins=[xsrc_d], outs=[xdst_d]), reads=[xsrc_d], writes=[xdst_d],
              is_dma=True, dma_slot="cc1", inc=1)
        P.dma("sp", xg, xdst_d.rearrange("(r p) f -> p r f", p=128), "xg")
        def combine_states():
            P.memset("dve", S, 0.0)
            P.memset("dve", Xcar, 0.0)
            om = cc[:, 0, 0:4]
            Dr = cc[:, 1, 0:4]
            Ar = cc[:, 2, 0:4]
            P.ts("dve", om, meta[:, 0:4], -1.0, 1.0, ALU.mult, ALU.add)
            for r in range(4):
                mr = meta[:, r:r + 1]
                P.act(Dr, xg[:, r, 512:516], AF.Exp)
                P.ts("dve", Ar, Dr, mr, om[:, r:r + 1], ALU.mult, ALU.add)
                P.ts("dve", sig, xg[:, r, 0:512], mr, None, ALU.mult)
                for h in range(4):
                    P.stt("dve", S[:, h, :], S[:, h, :], Ar[:, h:h + 1], sig[:, h * 128:(h + 1) * 128], ALU.mult, ALU.add)
                cmul("dve", Xold[:, 0, :], Xold[:, 1, :], A2048[:, 0, :], A2048[:, 1, :], Xcar[:, 0, :], Xcar[:, 1, :],
                     cc[:, 3, :], lf[:, 0:16])
                for ri in range(2):
                    P.tt("dve", Xold[:, ri, :], Xold[:, ri, :], xg[:, r, 516 + 16 * ri:532 + 16 * ri], ALU.add)
                    P.tt("dve", Xold[:, ri, :], Xold[:, ri, :], Xcar[:, ri, :], ALU.subtract)
                    P.stt("dve", Xcar[:, ri, :], Xold[:, ri, :], mr, Xcar[:, ri, :], ALU.mult, ALU.add)
            P.copy("dve", SbA, S)
            dbg("S0", S)
            dbg("X0", Xcar)


        A.off = b2_base
        sgB = A.alloc([TT], F32)
        xmb = [A.alloc([8, TT + 2], F32) for _ in range(2)]
        hfb = [A.alloc([8, TT + 2], BF16) for _ in range(2)]
        hid = A.alloc([22, TT], BF16)
        stg = [[A.alloc([TT + 2], F32) for _ in range(2)] for _ in range(2)]
        cgv = [A.alloc([2, TT], F32) for _ in range(2)]
        pb_ = A.alloc([2, TT], BF16)
        wpp = A.alloc([2, 1024], BF16)
        halo_a = A.alloc([44, 2], F32)
        wbufsB2 = [A.alloc([4096], BF16) for _ in range(5)]
        sq2 = A.alloc([8, 2], BF16)
        rs2 = A.alloc([2], F32)
        hp = hT
        B2ORDER = (1, 2, 3, 0)

        def b2_prep(idx):
            t = B2ORDER[idx]
            xm = xmb[idx % 2]
            hf = hfb[idx % 2]
            P.dma("sp", xm[:, :, 2:TT + 2], xmid_v[:, :, t * TT:(t + 1) * TT], "xm2_%d" % (idx % 2))
            if t == 1:
                P.dma("sp", xm[:, :, 0:2], xmid_v[:, :, t * TT - 2:t * TT], "xm2h")
            elif t == 0:
                hxv = hx.rearrange("p r (k c) -> p r k c", c=2)
                P.ts("dve", xm[:, :, 0:2], hxv[:, 0], meta[:, 4:5], None, ALU.mult)
                for r in range(1, 4):
                    P.stt("dve", xm[:, :, 0:2], hxv[:, r], meta[:, 4 + r:5 + r], xm[:, :, 0:2], ALU.mult, ALU.add)
            rmsnorm(xm[:, :, 2:TT + 2], gcol(P_GFFN), hf[:, :, 2:TT + 2], TT, sq, rs)
            if t in (1, 0):
                rmsnorm(xm[:, :, 0:2], gcol(P_GFFN), hf[:, :, 0:2], 2, sq2, rs2)

        for t in range(NT):
            if t > 0:
                b1_prefix(t)
            if t > 0:
                s5_chain()
            wqs = wq.next()
            for m in range(4):
                pq = bank()
                for k in range(8):
                    P.mm(pq, wqs[:, k, m * 128:(m + 1) * 128], hT[:, k, :], start=(k == 0), stop=(k == 7))
                P.act(qT[:, m, :], pq, AF.Silu)
            wog = wq.next()
            for m in range(4):
                pq = bank()
                for k in range(8):
                    P.mm(pq, wog[:, k, m * 128:(m + 1) * 128], hT[:, k, :], start=(k == 0), stop=(k == 7))
                P.act(ogT[:, m, :], pq, AF.Silu)
            if t == 0:
                combine_states()
                s5_chain()
            s5_out()
            wf = wq.next()
            wi = wq.next()
            hgrn_tile(wf, wi, True)
            if t == NT - 1:
                b2_prep(0)
            if t == 0:
                dbg("yhg", yhg)
                dbg("ys5", ys5)
                dbg("gf", gf)
            for half in range(2):
                wgh = wq.next()
                wgs = wq.next()
                for mm_ in range(4):
                    m = half * 4 + mm_
                    ms = slice(m * 128, (m + 1) * 128)
                    pa = bank()
                    for h in range(4):
                        P.mm(pa, wbh[:, h, ms], yhg[:, h, :], start=(h == 0), stop=(h == 3))
                    pb = bank()
                    for h in range(4):
                        P.mm(pb, wbs[:, h, ms], ys5[:, h, :], start=(h == 0), stop=(h == 3))
                    pc = bank()
                    for k in range(8):
                        P.mm(pc, wgh[:, k, mm_ * 128:(mm_ + 1) * 128], hT[:, k, :], start=(k == 0), stop=(k == 7))
                    pd = bank()
                    for k in range(8):
                        P.mm(pd, wgs[:, k, mm_ * 128:(mm_ + 1) * 128], hT[:, k, :], start=(k == 0), stop=(k == 7))
                    P.act(sg, pc, AF.Sigmoid)
                    P.act(sg2, pd, AF.Sigmoid)
                    P.tt("dve", mt1, sg, pa, ALU.mult)
                    P.tt("dve", sg2, sg2, pb, ALU.mult)
                    P.tt("dve", merged[:, m, :], mt1, sg2, ALU.add)
            load_x(t)
            for half in range(2):
                wo = wq.next()
                for mm_ in range(4):
                    m = half * 4 + mm_
                    po = bank()
                    for k in range(8):
                        P.mm(po, wo[:, k, mm_ * 128:(mm_ + 1) * 128], merged[:, k, :], start=(k == 0), stop=(k == 7))
                    P.tt("dve", xt[:, m, :], xt[:, m, :], po, ALU.add)
            for hh in range(2):
                P.dma("sp", xmid_v[:, 4 * hh:4 * hh + 4, t * TT:(t + 1) * TT], xt[:, 4 * hh:4 * hh + 4, :], "xm%d" % hh)
            if t == 0:
                dbg("xmid0", xt)
                dbg("merged0", merged)
            if t == NT - 1:
                P.dma("sp", hsrc_d.rearrange("p (k c) -> p k c", c=2), xt[:, :, TT - 2:TT], "hs")
                P.add("pool", lambda e: e.collective_compute("AllGather", ALU.bypass,
                                                              replica_groups=[[0, 1, 2, 3], [4, 5, 6, 7]],
                                                              ins=[hsrc_d], outs=[hdst_d]), reads=[hsrc_d], writes=[hdst_d],
                      is_dma=True, dma_slot="cc2", inc=1)
                P.dma("sp", hx, hdst_d.rearrange("(r p) f -> p r f", p=128), "hx")

        sg = sgB
        P.dma("pool", wpp, w_pp_v, "wpp")
        wq.set_bufs(wbufsB2)
        for idx, t in enumerate(B2ORDER):
            xm = xmb[idx % 2]
            hf = hfb[idx % 2]
            P.dma("pool", pb_, pT_v[:, :, t * TT:(t + 1) * TT], "pT")
            carry = (t in (2, 3))
            for j in range(22):
                if j % 4 == 0:
                    wg_ = wq.next()
                    wv_ = wq.next()
                sb_ = stg[j % 2]
                cb_ = cgv[j % 2]
                for gv in range(2):
                    jj = gv * 22 + j
                    pm = bank()
                    for k in range(8):
                        P.mm(pm, (wg_, wv_)[gv][:, k, (j % 4) * 128:(j % 4 + 1) * 128], hf[:, k, 2:TT + 2], start=(k == 0), stop=(k == 7))
                    if carry:
                        P.copy("act", sb_[gv][:, 0:2], halo_a[:, jj, :])
                    else:
                        ph = bank()
                        for k in range(8):
                            P.mm(ph[:, 0:2], (wg_, wv_)[gv][:, k, (j % 4) * 128:(j % 4 + 1) * 128], hf[:, k, 0:2], start=(k == 0), stop=(k == 7))
                        P.act(sb_[gv][:, 0:2], ph[:, 0:2], AF.Copy)
                    P.act(sb_[gv][:, 2:TT + 2], pm, AF.Copy)
                    P.act(cb_[:, gv, :], pm, AF.Identity, scale=convw[:, 2, jj:jj + 1], bias=convb[:, jj:jj + 1])
                    if t != 0:
                        P.copy("act", halo_a[:, jj, :], sb_[gv][:, TT:TT + 2])
                    P.stt("dve", cb_[:, gv, :], sb_[gv][:, 1:TT + 1], convw[:, 1, jj:jj + 1], cb_[:, gv, :], ALU.mult, ALU.add)
                    P.stt("dve", cb_[:, gv, :], sb_[gv][:, 0:TT], convw[:, 0, jj:jj + 1], cb_[:, gv, :], ALU.mult, ALU.add)
                P.act(cb_[:, 0, :], cb_[:, 0, :], AF.Gelu_apprx_tanh)
                P.tt("dve", hid[:, j, :], cb_[:, 0, :], cb_[:, 1, :], ALU.mult)
            if t == 1:
                dbg("hf1", hf)
                dbg("hid1", hid)
            if idx + 1 < NT:
                b2_prep(idx + 1)
            pos = [psb[m][:, :] for m in range(8)]
            for (j0, nj) in FGRP:
                wd = wq.next()
                for jl in range(nj):
                    j = j0 + jl
                    for m in range(8):
                        P.mm(pos[m], wd[:, jl, m * 128:(m + 1) * 128], hid[:, j, :], start=(j == 0), stop=(j == 21))
            for m in range(8):
                P.tt("dve", xm[:, m, 2:TT + 2], xm[:, m, 2:TT + 2], pos[m], ALU.add)
            if t == 1:
                dbg("x2_1", xm)
            rmsnorm(xm[:, :, 2:TT + 2], gcol(P_GPLE), hp, TT, sq, rs)
            for half in range(2):
                wpg = wq.next()
                for mm_ in range(4):
                    m = half * 4 + mm_
                    pg = bank()
                    for k in range(8):
                        P.mm(pg, wpg[:, k, mm_ * 128:(mm_ + 1) * 128], hp[:, k, :], start=(k == 0), stop=(k == 7))
                    pp = bank()
                    for k in range(2):
                        P.mm(pp, wpp[:, k, m * 128:(m + 1) * 128], pb_[:, k, :], start=(k == 0), stop=(k == 1))
                    P.act(sg, pg, AF.Sigmoid)
                    P.tt("dve", sg, sg, pp, ALU.mult)
                    P.tt("dve", xm[:, m, 2:TT + 2], xm[:, m, 2:TT + 2], sg, ALU.add)
            if t == 1:
                dbg("x3_1", xm)
            rmsnorm(xm[:, :, 2:TT + 2], gcol(P_GFIN), xt, TT, sq, rs)
            P.dma("sp", outT_v[:, :, t * TT:(t + 1) * TT], xt, "out")
        global _LASTP, _DBGAP
        _LASTP = P
        _DBGAP = dict(qT=qT, uT=uT, wre=wre)
        P.emit()
    return nc, dbg_out


def _consts():
    c = np.zeros((128, C_N), np.float32)
    s = np.arange(128)[:, None]
    t = np.arange(128)[None, :]
    same = (s // 64) == (t // 64)
    tri = (same & (s <= t)).astype(np.float32)
    ref = (same & ((s % 64) <= 32)).astype(np.float32)
    c[:, C_ID:C_ID + 128] = np.eye(128, dtype=np.float32)
    c[:, C_MQ:C_MQ + 128] = tri - ref
    c[:, C_TRI:C_TRI + 128] = tri
    c[:, C_IND:C_IND + 2] = (np.arange(128)[:, None] // 64 == np.arange(2)[None, :]).astype(np.float32)
    c[:, C_NMQ:C_NMQ + 128] = ref - tri
    c[:, C_ML:C_ML + 128] = same.astype(np.float32) - tri
    c[:, C_RM:C_RM + 4] = (np.arange(128)[:, None] // 32 == np.arange(4)[None, :]).astype(np.float32)
    c[:, C_ONES:C_ONES + 128] = 1.0
    return c


def _pair_layout(a):
    q = np.arange(16)
    ct, qp = q % 4, q // 4
    out = np.empty((2, 64, 16) + a.shape[2:], a.dtype)
    for g2 in range(2):
        g = 8 * ct + 2 * qp + g2
        out[g2] = np.moveaxis(a[g], 0, 1)
    return out.reshape((128, 16) + a.shape[2:])


_CACHE = {}


def _get_prog(debug=()):
    key = tuple(debug)
    if key not in _CACHE:
        _CACHE[key] = build(debug)
    return _CACHE[key]


def kernel(x, p, norm_mix_g, w_in, hg_lb_logits, hg_norm_g, s5_lambda_re, s5_lambda_im, s5_log_dt, s5_b_re, s5_b_im,
           s5_c_re, s5_c_im, s5_d, s5_glu_w, s5_glu_b, w_branch_hg, w_branch_s5, w_out, norm_ffn_g, w_up, conv_w, conv_b,
           w_down, norm_ple_g, w_ple_gate, w_ple_proj, norm_final_g, _debug=()):
    f = lambda a: np.ascontiguousarray(np.asarray(a, dtype=np.float32))
    x = f(x)
    p = f(p)
    nc, dbg_out = _get_prog(_debug)
    col8 = lambda g: f(g).reshape(8, 128).T
    prm = np.zeros((128, P_N), np.float32)
    prm[:, P_GMIX:P_GMIX + 8] = col8(norm_mix_g[0])
    prm[:, P_GFFN:P_GFFN + 8] = col8(norm_ffn_g[0])
    prm[:, P_GPLE:P_GPLE + 8] = col8(norm_ple_g[0])
    prm[:, P_GFIN:P_GFIN + 8] = col8(norm_final_g)
    prm[:, P_HGG] = f(hg_norm_g[0])
    prm[:, P_GLUB:P_GLUB + 4] = f(s5_glu_b[0]).reshape(4, 128).T
    prm[:, P_DSK:P_DSK + 4] = f(s5_d[0]).reshape(4, 128).T
    prm[:, P_CW:P_CW + 132] = f(conv_w[0]).reshape(3, 44, 128).transpose(2, 0, 1).reshape(128, 132)
    prm[:, P_CB:P_CB + 44] = f(conv_b[0]).reshape(44, 128).T
    prq = np.zeros((128, Q_N), np.float32)
    lb = f(hg_lb_logits)
    prq[:, Q_L0:Q_L0 + 512] = lb[0][None, :]
    prq[:, Q_L1:Q_L1 + 512] = lb[1][None, :]
    prq[:, Q_LRE:Q_LRE + 16] = _pair_layout(f(s5_lambda_re[0]))
    prq[:, Q_LIM:Q_LIM + 16] = _pair_layout(f(s5_lambda_im[0]))
    prq[:, Q_LDT:Q_LDT + 16] = _pair_layout(np.repeat(f(s5_log_dt[0])[:, None], 64, axis=1))
    prq[:, Q_BRE:Q_BRE + 256] = _pair_layout(f(s5_b_re[0])).reshape(128, 256)
    prq[:, Q_BIM:Q_BIM + 256] = _pair_layout(f(s5_b_im[0])).reshape(128, 256)
    prq[:, Q_CRE:Q_CRE + 256] = _pair_layout(f(s5_c_re[0]).transpose(0, 2, 1)).reshape(128, 256)
    prq[:, Q_CIM:Q_CIM + 256] = _pair_layout(f(s5_c_im[0]).transpose(0, 2, 1)).reshape(128, 256)
    cst = _consts()
    shared = {
        "cst": cst, "prq": prq, "w_in": f(w_in[0]), "w_bh": f(w_branch_hg[0]), "w_bs": f(w_branch_s5[0]),
        "w_glu": f(s5_glu_w[0]), "w_out": f(w_out[0]), "w_up": f(w_up[0]), "w_down": f(w_down[0]),
        "w_pg": f(w_ple_gate[0]), "w_pp": f(w_ple_proj[0]),
    }
    in_maps = []
    for c in range(NCORES):
        b, j = c // 4, c % 4
        pr = prm.copy()
        for r in range(4):
            pr[:, P_META + r] = 1.0 if r < j else 0.0
            pr[:, P_META + 4 + r] = 1.0 if r == j - 1 else 0.0
        d = dict(shared)
        d["prm"] = pr
        d["xT"] = np.ascontiguousarray(x[b, j * TOK:(j + 1) * TOK, :].T)
        d["pT"] = np.ascontiguousarray(p[0, b, j * TOK:(j + 1) * TOK, :].T)
        in_maps.append(d)
    res = run_bass_kernel_spmd(nc, in_maps, core_ids=list(range(NCORES)))
    out = np.empty((2, 8192, 1024), np.float32)
    for c in range(NCORES):
        b, j = c // 4, c % 4
        out[b, j * TOK:(j + 1) * TOK, :] = res.results[c]["outT"].T
    if _debug:
        kernel.last_debug = [{k: res.results[c]["dbg_" + k] for k in dbg_out} for c in range(NCORES)]
    return out
```
